# Optimizing a Trainium2 kernel written in Bass

```python
import jax
import jax.numpy as jnp
from jax import lax
import numpy as np

D_MODEL = 1024
BATCH = 32
SEQ = 2048
DEPTH = 1

CHUNK = 64
HEAD_DIM = 64
N_HEADS_A = 8
LEFT_CHUNKS_A = 8
MAX_REL = 128
N_HEADS_B = 8
N_KV_B = 2
WINDOW_B = 128
LEFT_CHUNKS_B = WINDOW_B // CHUNK
ROT_DIM = HEAD_DIM // 4
ROPE_THETA = 500000.0
D_FF = 2816
EPS = 1e-6
NEG_INF = -1e30

W_A = N_HEADS_A * HEAD_DIM
W_QB = N_HEADS_B * HEAD_DIM
W_KVB = N_KV_B * HEAD_DIM
D_IN = 3 * W_A + W_QB + 2 * W_KVB
D_MIX = W_A + W_QB

kernel_name = "hybrid_chunk_stream_encoder_layer"


def rms_norm(x, g):
    xf = x.astype(jnp.float32)
    y = xf * lax.rsqrt(jnp.mean(xf * xf, axis=-1, keepdims=True) + EPS)
    return (y * g.astype(jnp.float32)).astype(x.dtype)


def swiglu(x, w_gate, w_up, w_down):
    return (jax.nn.silu(x @ w_gate) * (x @ w_up)) @ w_down


def partial_rope(x, pos):
    half = ROT_DIM // 2
    inv_freq = jnp.power(jnp.float32(ROPE_THETA), -jnp.arange(half, dtype=jnp.float32) * (2.0 / ROT_DIM))
    ang = pos.astype(jnp.float32)[..., None] * inv_freq
    cos = jnp.cos(ang)[:, :, None, :]
    sin = jnp.sin(ang)[:, :, None, :]
    xr = x[..., :ROT_DIM].astype(jnp.float32)
    x1, x2 = xr[..., :half], xr[..., half:]
    rot = jnp.concatenate([x1 * cos - x2 * sin, x2 * cos + x1 * sin], axis=-1).astype(x.dtype)
    return jnp.concatenate([rot, x[..., ROT_DIM:]], axis=-1)


def rel_position_bias(table):
    band = (LEFT_CHUNKS_A + 1) * CHUNK
    i = jnp.arange(CHUNK)[:, None]
    j = jnp.arange(band)[None, :]
    dist = LEFT_CHUNKS_A * CHUNK + i - j
    idx = jnp.clip(dist, -MAX_REL, MAX_REL) + MAX_REL
    return table[:, idx]


def band_attention(q, k, v, left_chunks, bias, sinks):
    b, s, hkv, g, dh = q.shape
    n_chunks = s // CHUNK
    band = (left_chunks + 1) * CHUNK
    pad = left_chunks * CHUNK
    kp = jnp.pad(k, ((0, 0), (pad, 0), (0, 0), (0, 0)))
    vp = jnp.pad(v, ((0, 0), (pad, 0), (0, 0), (0, 0)))
    scale = dh ** -0.5
    key_off = jnp.arange(band) - pad

    def one_chunk(c):
        start = c * CHUNK
        qc = lax.dynamic_slice_in_dim(q, start, CHUNK, axis=1).astype(jnp.float32)
        kc = lax.dynamic_slice_in_dim(kp, start, band, axis=1).astype(jnp.float32)
        vc = lax.dynamic_slice_in_dim(vp, start, band, axis=1).astype(jnp.float32)
        logits = jnp.einsum('bqhgd,bkhd->bhgqk', qc, kc) * scale
        if bias is not None:
            logits = logits + bias.astype(jnp.float32)
        valid = (start + key_off) >= 0
        logits = jnp.where(valid, logits, NEG_INF)
        if sinks is None:
            p = jax.nn.softmax(logits, axis=-1)
        else:
            sk = sinks.astype(jnp.float32)[None, :, :, None, None]
            m = jnp.maximum(jnp.max(logits, axis=-1, keepdims=True), sk)
            e = jnp.exp(logits - m)
            p = e / (jnp.sum(e, axis=-1, keepdims=True) + jnp.exp(sk - m))
        out = jnp.einsum('bhgqk,bkhd->bqhgd', p, vc)
        return out.astype(q.dtype)

    outs = lax.map(one_chunk, jnp.arange(n_chunks))
    return jnp.moveaxis(outs, 0, 1).reshape(b, s, hkv * g, dh)


def hybrid_mixer(u, positions, w_in, rel_table_a, sinks_b, w_out):
    b, s, _ = u.shape
    proj = u @ w_in
    qa, ka, va, qb, kb, vb = jnp.split(
        proj, [W_A, 2 * W_A, 3 * W_A, 3 * W_A + W_QB, 3 * W_A + W_QB + W_KVB], axis=-1)
    qa = qa.reshape(b, s, N_HEADS_A, 1, HEAD_DIM)
    ka = ka.reshape(b, s, N_HEADS_A, HEAD_DIM)
    va = va.reshape(b, s, N_HEADS_A, HEAD_DIM)
    bias_a = rel_position_bias(rel_table_a)[:, None]
    oa = band_attention(qa, ka, va, LEFT_CHUNKS_A, bias_a, None).reshape(b, s, W_A)
    qb = partial_rope(qb.reshape(b, s, N_HEADS_B, HEAD_DIM), positions)
    qb = qb.reshape(b, s, N_KV_B, N_HEADS_B // N_KV_B, HEAD_DIM)
    kb = partial_rope(kb.reshape(b, s, N_KV_B, HEAD_DIM), positions)
    vb = vb.reshape(b, s, N_KV_B, HEAD_DIM)
    sk = sinks_b.reshape(N_KV_B, N_HEADS_B // N_KV_B)
    ob = band_attention(qb, kb, vb, LEFT_CHUNKS_B, None, sk).reshape(b, s, W_QB)
    return jnp.concatenate([oa, ob], axis=-1) @ w_out


def setup_inputs(seed: int = 0) -> dict:
    key = jax.random.key(seed)
    ks = jax.random.split(key, 20)
    f32 = jnp.float32

    def nrm(k, shape, scale):
        return jax.random.normal(k, shape, f32) * scale

    def gain(k):
        return 1.0 + 0.05 * jax.random.normal(k, (DEPTH, D_MODEL), f32)

    x = jax.random.normal(ks[0], (BATCH, SEQ, D_MODEL), f32)
    offset = jax.random.randint(ks[1], (BATCH, 1), 0, 256) * CHUNK
    positions = (offset + jnp.arange(SEQ, dtype=jnp.int32)[None, :]).astype(jnp.int32)
    return {
        "x": x,
        "positions": positions,
        "ffn1_pre_g": gain(ks[2]),
        "ffn1_w_gate": nrm(ks[3], (DEPTH, D_MODEL, D_FF), D_MODEL ** -0.5),
        "ffn1_w_up": nrm(ks[4], (DEPTH, D_MODEL, D_FF), D_MODEL ** -0.5),
        "ffn1_w_down": nrm(ks[5], (DEPTH, D_FF, D_MODEL), D_FF ** -0.5),
        "ffn1_post_g": gain(ks[6]),
        "mix_pre_g": gain(ks[7]),
        "w_in": nrm(ks[8], (DEPTH, D_MODEL, D_IN), D_MODEL ** -0.5),
        "rel_bias_a": nrm(ks[9], (DEPTH, N_HEADS_A, 2 * MAX_REL + 1), 0.1),
        "sinks_b": nrm(ks[10], (DEPTH, N_HEADS_B), 0.5),
        "w_out": nrm(ks[11], (DEPTH, D_MIX, D_MODEL), D_MIX ** -0.5),
        "mix_post_g": gain(ks[12]),
        "ffn2_pre_g": gain(ks[13]),
        "ffn2_w_gate": nrm(ks[14], (DEPTH, D_MODEL, D_FF), D_MODEL ** -0.5),
        "ffn2_w_up": nrm(ks[15], (DEPTH, D_MODEL, D_FF), D_MODEL ** -0.5),
        "ffn2_w_down": nrm(ks[16], (DEPTH, D_FF, D_MODEL), D_FF ** -0.5),
        "ffn2_post_g": gain(ks[17]),
        "final_g": gain(ks[18]),
    }


def reference(x, positions, ffn1_pre_g, ffn1_w_gate, ffn1_w_up, ffn1_w_down, ffn1_post_g,
              mix_pre_g, w_in, rel_bias_a, sinks_b, w_out, mix_post_g,
              ffn2_pre_g, ffn2_w_gate, ffn2_w_up, ffn2_w_down, ffn2_post_g, final_g):
    h = x
    for l in range(DEPTH):
        f = swiglu(rms_norm(h, ffn1_pre_g[l]), ffn1_w_gate[l], ffn1_w_up[l], ffn1_w_down[l])
        h = h + 0.5 * rms_norm(f, ffn1_post_g[l])
        m = hybrid_mixer(rms_norm(h, mix_pre_g[l]), positions, w_in[l], rel_bias_a[l],
                         sinks_b[l], w_out[l])
        h = h + rms_norm(m, mix_post_g[l])
        f = swiglu(rms_norm(h, ffn2_pre_g[l]), ffn2_w_gate[l], ffn2_w_up[l], ffn2_w_down[l])
        h = h + 0.5 * rms_norm(f, ffn2_post_g[l])
        h = rms_norm(h, final_g[l])
    return h
```

```python
import numpy as np
from contextlib import ExitStack
import concourse.bass as bass
import concourse.mybir as mybir
from concourse.bass_utils import run_bass_kernel_spmd

F32 = mybir.dt.float32
BF16 = mybir.dt.bfloat16
I32 = mybir.dt.int32
AF = mybir.ActivationFunctionType
ALU = mybir.AluOpType

NCORES = 8
SEQ = 2048
D = 1024
DFF = 2816
T = 512
NFC = 22
NDC = 8
SEQ_PER_CORE = 4
TILES_PER_SEQ = SEQ // T
EPS = 1e-6
NEG = -30000.0
WINB_COLS = 3200
C_QA, C_KA, C_QB, C_QBS, C_KB4, C_VA, C_VB = 0, 512, 1024, 1536, 2048, 2560, 3072
TWO_PI = 2.0 * np.pi
CW1 = float(np.float32(6.28125))
CW2 = float(np.float32(TWO_PI - 6.28125))
OPT_FINE_WD = False
OPT_BAL_RECIP = True
OPT_SPLIT_GU = True
PI_LO = 3.1415925


class Ev:
    __slots__ = ("src", "c")

    def __init__(self, src, c):
        self.src = src
        self.c = c


class Res:
    __slots__ = ("name", "w", "r", "excl")

    def __init__(self, name="", excl=False):
        self.name = name
        self.w = None
        self.r = {}
        self.excl = excl


class Eng:
    def __init__(self, name, handle, sem):
        self.name = name
        self.h = handle
        self.sem = sem
        self.cnt = 0
        self.seen = {}
        self.thunks = []


class DSem:
    def __init__(self, name, sem):
        self.name = name
        self.sem = sem
        self.cnt = 0


class K:
    def __init__(self):
        self.engs = []

    def _deps(self, eng, reads, writes):
        need = {}

        def add(ev):
            if ev is None:
                return
            if need.get(ev.src, 0) < ev.c:
                need[ev.src] = ev.c

        for r in reads:
            add(r.w)
        for w in writes:
            add(w.w)
            for ev in w.r.values():
                add(ev)
        for src, c in need.items():
            if eng.seen.get(src, 0) < c:
                eng.seen[src] = c
                eng.h.wait_ge(src.sem, c)

    def op(self, eng, fn, reads=(), writes=()):
        ex = [r for r in reads if r.excl]
        if ex:
            writes = list(writes) + ex
        self._deps(eng, reads, writes)
        eng.cnt += 1
        ev = Ev(eng, eng.cnt)
        fn(eng.h).then_inc(eng.sem, 1)
        for r in reads:
            r.r[eng] = ev
        for w in writes:
            w.w = ev
            w.r = {}
        return ev

    def dma(self, eng, dsem, fn, reads=(), writes=()):
        self._deps(eng, reads, writes)
        dsem.cnt += 16
        ev = Ev(dsem, dsem.cnt)
        fn(eng.h).then_inc(dsem.sem, 16)
        for r in reads:
            r.r[dsem] = ev
        for w in writes:
            w.w = ev
            w.r = {}
        return ev

    def wait_all(self, eng, evs):
        for ev in evs:
            if ev is not None and eng.seen.get(ev.src, 0) < ev.c:
                eng.seen[ev.src] = ev.c
                eng.h.wait_ge(ev.src.sem, ev.c)


class StopBuild(Exception):
    pass


def build(n_tiles=16, stop=None):
    nc = bass.Bass("TRN2", target_bir_lowering=False)
    es = ExitStack()

    def checkpoint(name):
        if stop == name:
            raise StopBuild()
    try:
        _build_body(nc, es, n_tiles, checkpoint)
    except StopBuild:
        pass
    es.close()
    return nc


def _build_body(nc, es, n_tiles, checkpoint):

    def dram(name, shape, dt, kind="Internal"):
        return nc.dram_tensor(name, shape, dt, kind=kind)

    x_d = dram("x", [SEQ_PER_CORE, SEQ, D], F32, "ExternalInput")
    pos_d = dram("pos", [SEQ_PER_CORE, SEQ], I32, "ExternalInput")
    wg_d = [dram(f"wg{i}", [D, DFF], F32, "ExternalInput") for i in range(2)]
    wu_d = [dram(f"wu{i}", [D, DFF], F32, "ExternalInput") for i in range(2)]
    wd_d = [dram(f"wd{i}", [DFF, D], F32, "ExternalInput") for i in range(2)]
    win_d = dram("win", [D, 2304], F32, "ExternalInput")
    wout_d = dram("wout", [D, D], F32, "ExternalInput")
    gains_d = dram("gains", [7, D], F32, "ExternalInput")
    relb_d = dram("relb", [8, 257], F32, "ExternalInput")
    sinks_d = dram("sinks", [1, 8], F32, "ExternalInput")
    consts_d = dram("consts", [128, 264], F32, "ExternalInput")
    out_d = dram("out", [SEQ_PER_CORE, SEQ, D], F32, "ExternalOutput")
    gu_s = [dram(f"gu_s{i}", [11, 128, 4096], BF16) for i in range(2)]
    wd_s = [dram(f"wd_s{i}", [128, NFC * D], BF16) for i in range(2)]
    win_s = dram("win_s", [128, NDC, WINB_COLS], BF16)
    wout_s = dram("wout_s", [128, NDC, D], BF16)
    gt_s = dram("gt_s", [8, 384], F32)

    k = K()
    sem_i = [0]

    def newsem(name):
        sem_i[0] += 1
        return es.enter_context(nc.semaphore(f"{name}_{sem_i[0]}"))

    PE = Eng("pe", nc.tensor, newsem("pe"))
    ACT = Eng("act", nc.scalar, newsem("act"))
    DVE = Eng("dve", nc.vector, newsem("dve"))
    POOL = Eng("pool", nc.gpsimd, newsem("pool"))
    SP = Eng("sp", nc.sync, newsem("sp"))
    engs = [PE, ACT, DVE, POOL, SP]

    def dsem(name):
        return DSem(name, newsem(name))

    def barrier():
        evs = [Ev(e, e.cnt) for e in engs if e.cnt > 0] + [Ev(d, d.cnt) for d in all_dsems if d.cnt > 0]
        for e in engs:
            k.wait_all(e, evs)

    all_dsems = []

    def mk_dsem(name):
        d = dsem(name)
        all_dsems.append(d)
        return d

    ps_all = es.enter_context(nc.psum_tensor("ps_all", [128, 8 * 512], F32))

    class Bank:
        def __init__(self, t, b, w):
            self.t, self.b, self.w = t, b, w

        def __getitem__(self, key):
            p, c = key
            c0 = 0 if c.start is None else c.start
            c1 = self.w if c.stop is None else c.stop
            return self.t[p, self.b * self.w + c0: self.b * self.w + c1]

        def bitcast(self, dt):
            return Bank(self.t.bitcast(dt), self.b, self.w * 2)

    ps = [Bank(ps_all, i, 512) for i in range(8)]
    ps_res = [Res(f"ps{i}", excl=True) for i in range(8)]
    MM_BANKS = [0, 1, 2, 3]
    ST_BANKS = [4, 5, 6, 7]
    TR_BANKS = [4, 5]
    O_BANKS = [0, 1, 2, 3]
    rot = {"mm": 0, "st": 0, "o": 0, "tr": 0, "mm8": 0}

    def pair_ap(b, ncols):
        return ps_all[:, b * 512:(b + 2) * 512].rearrange("p (two n) -> p two n", two=2)[:, :, 0:ncols]

    def next_bank(pool):
        banks = {"mm": MM_BANKS, "st": ST_BANKS, "o": O_BANKS, "tr": TR_BANKS, "mm8": [6, 7, 0, 1, 2, 3]}[pool]
        b = banks[rot[pool] % len(banks)]
        rot[pool] += 1
        return b

    def v3(a, b):
        return lambda tns: tns[:, 0:a * b].rearrange("p (a b) -> p a b", a=a)

    def v3s(a, b, lo, hi):
        return lambda tns: tns[:, 0:a * b].rearrange("p (a b) -> p a b", a=a)[:, :, lo:hi]

    def flat(n):
        return lambda tns: tns[:, 0:n]

    def head_swap_casts(n_heads, src_off, dst_off, W_src, W_dst):
        res = []
        for (d0, s0, n) in ((0, 8, 8), (8, 0, 8), (16, 16, 48)):
            def ov(tns, d0=d0, n=n):
                a = tns[:, 0:8 * W_dst].rearrange("p (a b) -> p a b", a=8)[:, :, dst_off:dst_off + n_heads * 64]
                return a.rearrange("p a (h e) -> p a h e", h=n_heads)[:, :, :, d0:d0 + n]

            def iv(tns, s0=s0, n=n):
                a = tns[:, 0:8 * W_src].rearrange("p (a b) -> p a b", a=8)[:, :, src_off:src_off + n_heads * 64]
                return a.rearrange("p a (h e) -> p a h e", h=n_heads)[:, :, :, s0:s0 + n]
            res.append((ov, iv))
        return res

    wgv = [wg_d[f].rearrange("(dc p) f -> p dc f", p=128) for f in range(2)]
    wuv = [wu_d[f].rearrange("(dc p) f -> p dc f", p=128) for f in range(2)]
    wdv = [wd_d[f].rearrange("(fc p) d -> p fc d", p=128) for f in range(2)]
    winv = win_d.rearrange("(dc p) f -> p dc f", p=128)
    woutv = wout_d.rearrange("(c p) d -> p c d", p=128)
    gt_sem = mk_dsem("gt")
    gt_r = Res()
    with nc.allow_non_contiguous_dma(reason="one-off tiny table build"):
        k.dma(SP, gt_sem, lambda h: h.dma_start(out=gt_s[:, 0:256], in_=relb_d[:, 1:257]), [], [gt_r])
        k.dma(SP, gt_sem, lambda h: h.dma_start(out=bass.AP(gt_s, 256, [[384, 8], [1, 128], [1, 1]]),
                                                in_=bass.AP(relb_d, 256, [[257, 8], [0, 128], [1, 1]])), [], [gt_r])
    barrier()
    checkpoint("prep")

    def sb(name, shape, dt):
        return es.enter_context(nc.sbuf_tensor(name, shape, dt))

    h_t = [sb(f"h{i}", [128, 4, D], F32) for i in range(2)]
    h_r = [[Res(f"h{i}_{tb}") for tb in range(4)] for i in range(2)]
    xn_tok = [sb(f"xntok{i}", [128, D], BF16) for i in range(2)]
    xn_tok_r = [Res() for _ in range(2)]
    xnT = sb("xnT", [128, NDC, T], BF16)
    xnT_r = [Res(f"xnT{tb}") for tb in range(4)]
    U = sb("U", [128, NFC * T], BF16)
    act_v = U[:, :].rearrange("p (a b) -> p a b", a=NFC)
    U_r = [Res(f"U{i}") for i in range(NFC)]
    QaT = U[:, 0:2048].rearrange("p (a b) -> p a b", a=4)
    QbT = U[:, 2048:4096].rearrange("p (a b) -> p a b", a=4)
    OT = U[:, 4096:8192].rearrange("p (a b) -> p a b", a=8)
    PT = [U[:, 8192 + 512 * i: 8192 + 512 * (i + 1)] for i in range(6)]
    QaT_r, QbT_r, OT_r, PT_r = U_r[0:4], U_r[4:8], U_r[8:16], U_r[16:22]
    ring = [sb(f"ring{i}", [128, 4096], BF16) for i in range(3)]
    ring_r = [Res(f"ring{i}") for i in range(3)]
    ring_sem = [mk_dsem(f"ring{i}") for i in range(3)]
    Wd = sb("Wd", [128, NFC * D], BF16)
    Wd_r = Res("Wd")
    wd_sem = mk_dsem("wdl")
    wd_sem_sw = mk_dsem("wdlsw")
    ring_sem_sw = [mk_dsem(f"ringsw{i}") for i in range(3)]
    kv1 = sb("kv1", [128, 4096], F32)
    kv1_r = Res("kv1")
    kv1b = kv1.bitcast(BF16)
    KaT = [sb("KaT0", [128, 4, T], BF16), kv1b[:, 4096:6144].rearrange("p (a b) -> p a b", a=4)]
    KaT_r = [[Res() for _ in range(4)] for _ in range(2)]
    KbT = [sb("KbT0", [128, 2, T], BF16), kv1b[:, 6144:7168].rearrange("p (a b) -> p a b", a=2)]
    KbT_r = [[Res() for _ in range(2)] for _ in range(2)]
    Va = [sb("Va0", [128, 4, 1024], BF16), kv1b[:, 0:4096].rearrange("p (a b) -> p a b", a=4)]
    Va_r = [[Res() for _ in range(4)] for _ in range(2)]
    Vb = [sb(f"Vb{i}", [128, 4, 384], BF16) for i in range(2)]
    Vb_r = [[Res() for _ in range(4)] for _ in range(2)]
    tmpA = [sb(f"tmpA{i}", [128, 1024], F32) for i in range(2)]
    tmpA_r = [Res(f"tmpA{i}") for i in range(2)]
    Ct = sb("Ct", [128, T], F32)
    St = sb("St", [128, T], F32)
    CS_r = Res("CS")
    rsc = sb("rsc", [128, 1024], F32)
    rsc_r = Res("rsc")
    posi = rsc[:, 512:1024].bitcast(I32)
    posi_r = rsc_r
    junk = rsc[:, 0:256].bitcast(BF16)
    junk2 = rsc[:, 0:512].bitcast(BF16)
    junk_r = rsc_r
    pos_sem = mk_dsem("pos")
    gpost = [sb(f"gpost{i}", [128, D], F32) for i in range(4)]
    gpre = sb("gpre", [128, 24], F32)
    BT = sb("BT", [128, 8, 256], BF16)
    ident = sb("ident", [128, 128], BF16)
    antiI = sb("antiI", [128, 128], BF16)
    cst = sb("cst", [128, 8], F32)
    esink = sb("esink", [128, 8], F32)
    epsb = sb("epsb", [128, 1], F32)
    NACC = 0
    NROT = 48
    stats = sb("stats", [128, NACC + NROT], F32)
    rot_r = [Res() for _ in range(NROT)]
    const_r = Res("const")
    x_sem = [mk_dsem(f"xld{i}") for i in range(2)]
    out_sem = [mk_dsem(f"ost{i}") for i in range(2)]
    su_sem = mk_dsem("setup")

    su_res = []

    def setup_dma(out_ap, in_ap, res):
        with nc.allow_non_contiguous_dma(reason="one-off small constant loads"):
            k.dma(SP, su_sem, lambda h: h.dma_start(out=out_ap, in_=in_ap), [], [res])
        su_res.append(res)

    cst_r = Res()
    setup_dma(cst[:, :], consts_d[:, 256:264], cst_r)
    idst_r = h_r[0][0]
    setup_dma(h_t[0][:, 0, 0:256], consts_d[:, 0:256], idst_r)
    gp_r = [Res() for _ in range(4)]
    for i, gi in enumerate((1, 3, 5, 6)):
        setup_dma(gpost[i][:, :], bass.AP(gains_d, gi * D, [[0, 128], [1, D]]), gp_r[i])
    gpre_r = Res()
    with nc.allow_non_contiguous_dma(reason="tiny one-off gain vector transposed load"):
        for i, gi in enumerate((0, 2, 4)):
            k.dma(SP, su_sem, lambda h, i=i, gi=gi: h.dma_start(
                out=bass.AP(gpre, i * 8, [[24, 128], [1, 8], [1, 1]]), in_=bass.AP(gains_d, gi * D, [[1, 128], [128, 8], [1, 1]])), [], [gpre_r])
    su_res.append(gpre_r)
    sk_r = Res()
    setup_dma(esink[:, :], bass.AP(sinks_d, 0, [[0, 128], [1, 8]]), sk_r)
    hank = tmpA
    cc = sb("cc", [128, 8], F32)
    cc_r = Res()
    setup_dma(cc[:, :], bass.AP(relb_d, 256, [[0, 128], [257, 8]]), cc_r)
    hk_r = [Res(), Res()]
    for half in range(2):
        setup_dma(tmpA[half][:, :].rearrange("p (h q) -> p h q", h=4),
                  bass.AP(gt_s, half * 4 * 384, [[1, 128], [384, 4], [1, 256]]), hk_r[half])
    for r_ in su_res:
        r_.w = Ev(su_sem, su_sem.cnt)
    k.op(DVE, lambda h: h.tensor_copy(out=ident[:, :], in_=h_t[0][:, 0, 0:128]), [idst_r], [const_r])
    k.op(DVE, lambda h: h.tensor_copy(out=antiI[:, :], in_=h_t[0][:, 0, 128:256]), [idst_r], [const_r])
    k.op(POOL, lambda h: h.memset(epsb[:, :], EPS), [], [const_r])
    k.op(POOL, lambda h: h.memset(stats[:, :], 0.0), [], [const_r])
    for i in range(2):
        k.op(POOL, lambda h, i=i: h.memset(Va[i][:, :, :], 1.0), [], [r for r in Va_r[i]])
        k.op(POOL, lambda h, i=i: h.memset(Vb[i][:, :, :], 1.0), [], [r for r in Vb_r[i]])
    k.op(ACT, lambda h: h.activation(out=esink[:, :], in_=esink[:, :], func=AF.Exp), [sk_r], [sk_r])
    k.op(DVE, lambda h: h.tensor_scalar(out=gpost[0][:, :], in0=gpost[0][:, :], scalar1=0.5, scalar2=None, op0=ALU.mult), [gp_r[0]], [gp_r[0]])
    k.op(DVE, lambda h: h.tensor_scalar(out=gpost[2][:, :], in0=gpost[2][:, :], scalar1=0.5, scalar2=None, op0=ALU.mult), [gp_r[2]], [gp_r[2]])
    hb16 = PT
    for hd in range(8):
        half, j = hd // 4, hd % 4
        src = tmpA[half][:, j * 256:(j + 1) * 256]
        dst = PT[hd // 2][:, (hd % 2) * 256:(hd % 2 + 1) * 256]
        k.op(DVE, lambda h, src=src, dst=dst, hd=hd: h.tensor_scalar(
            out=dst, in0=src, scalar1=cc[:, hd:hd + 1], scalar2=8.0, op0=ALU.subtract, op1=ALU.mult),
            [hk_r[half], cc_r], [PT_r[hd // 2]])
    for hd in range(8):
        b = next_bank("mm")
        dst = PT[hd // 2][:, (hd % 2) * 256:(hd % 2 + 1) * 256]
        k.op(PE, lambda h, b=b, dst=dst: h.matmul(ps[b][:, 0:256], antiI[:, :], dst, start=True, stop=True),
             [const_r, PT_r[hd // 2]], [ps_res[b]])
        k.op(DVE, lambda h, b=b, hd=hd: h.tensor_copy(out=BT[:, hd, :], in_=ps[b][:, 0:256]), [ps_res[b]], [const_r])
    k.op(POOL, lambda h: h.memset(BT[64:128, :, 0:64], NEG), [const_r], [const_r])
    k.op(ACT, lambda h: h.activation(out=BT[:, :, :], in_=BT[:, :, :], func=AF.Exp, scale=0.125), [const_r], [const_r])
    barrier()
    checkpoint("setup")

    pieces = []
    piece_loaded = [0]
    piece_used = [0]
    scr = {}

    def scr_res(key):
        if key not in scr:
            scr[key] = Res(str(key))
        return scr[key]
    wd_scr_r = [[Res() for _ in range(11)] for _ in range(2)]

    def piece_src(kind, f=None, p=None):
        r_ = scr_res((kind, f, p))
        if kind == "gu":
            return gu_s[f][p], (lambda tns: tns[:, :]), r_
        cols = {"qa": (C_QA, 512), "ka": (C_KA, 512), "qb": (C_QB, 512), "qbs": (C_QBS, 512), "kb4": (C_KB4, 512),
                "va": (C_VA, 512), "vb": (C_VB, 128)}
        if kind in cols:
            c0, w = cols[kind]
            return win_s[:, :, c0:c0 + w], v3(8, w), r_
        if kind == "wo":
            return wout_s[:, :, p * 512:(p + 1) * 512], v3(8, 512), r_
        raise ValueError(kind)

    def tile_piece_list():
        lst = [("gu", 0, p) for p in range(11)]
        lst += [("va", None, None), ("vb", None, None), ("qa", None, None), ("ka", None, None), ("qb", None, None),
                ("qbs", None, None), ("kb4", None, None)]
        lst += [("wo", None, 0), ("wo", None, 1)]
        lst += [("gu", 1, p) for p in range(11)]
        return lst
    NP0 = len(tile_piece_list())

    for t in range(n_tiles):
        for it in tile_piece_list():
            pieces.append(piece_src(*it))

    order0 = [("qbs", None, None), ("kb4", None, None)]
    jit_pos = {it: j for j, it in enumerate(order0)}

    def jit_spec(kind, f, p):
        ident_v = (lambda d: d)
        if kind == "gu":
            lo, hi = p * 256, (p + 1) * 256
            return ([(lambda tns: tns[:, 0:2048].rearrange("p (a b) -> p a b", a=8), wgv[f][:, :, lo:hi]),
                     (lambda tns: tns[:, 2048:4096].rearrange("p (a b) -> p a b", a=8), wuv[f][:, :, lo:hi])],
                    [(flat(4096), flat(4096))], [(gu_s[f][p], flat(4096), scr_res((kind, f, p)))])
        if kind == "wd":
            f0, f1 = 4 * p, min(4 * p + 4, NFC)
            n = (f1 - f0) * D
            return ([(v3(f1 - f0, D), wdv[f][:, f0:f1, :])], [(ident_v, flat(n))],
                    [(wd_s[f][:, f0 * D:f1 * D], ident_v, wd_scr_r[f][p])])
        if kind in ("qa", "ka", "qb", "va"):
            c_src = {"qa": 0, "ka": 512, "qb": 1536, "va": 1024}[kind]
            c_dst = {"qa": C_QA, "ka": C_KA, "qb": C_QB, "va": C_VA}[kind]
            return ([(v3(8, 512), winv[:, :, c_src:c_src + 512])], [(flat(4096), flat(4096))],
                    [(win_s[:, :, c_dst:c_dst + 512], v3(8, 512), scr_res((kind, f, p)))])
        if kind == "qbs":
            return ([(v3(8, 512), winv[:, :, 1536:2048])], head_swap_casts(8, 0, 0, 512, 512),
                    [(win_s[:, :, C_QBS:C_QBS + 512], v3(8, 512), scr_res((kind, f, p)))])
        if kind == "kb4":
            cs_ = [(v3s(8, 512, 0, 128), v3(8, 128))]
            cs_ += head_swap_casts(2, 0, 128, 128, 512)
            cs_ += [(v3s(8, 512, 256, 320), v3s(8, 128, 64, 128)), (v3s(8, 512, 320, 384), v3s(8, 128, 0, 64))]
            cs_ += head_swap_casts(1, 64, 384, 128, 512) + head_swap_casts(1, 0, 448, 128, 512)
            return ([(v3(8, 128), winv[:, :, 2048:2176])], cs_,
                    [(win_s[:, :, C_KB4:C_KB4 + 512], v3(8, 512), scr_res((kind, f, p)))])
        if kind == "vb":
            return ([(v3(8, 128), winv[:, :, 2176:2304])], [(flat(1024), flat(1024))],
                    [(win_s[:, :, C_VB:C_VB + 128], v3(8, 128), scr_res((kind, f, p)))])
        if kind == "wo":
            return ([(v3(8, 512), woutv[:, :, p * 512:(p + 1) * 512])], [(flat(4096), flat(4096))],
                    [(wout_s[:, :, p * 512:(p + 1) * 512], v3(8, 512), scr_res((kind, f, p)))])
        raise ValueError(kind)

    stage_ap = [h_t[1][:, :, :].rearrange("p a b -> p (a b)"), kv1[:, :]]
    stage_res = [h_r[1], [kv1_r]]
    jit_sem = [mk_dsem("jl0"), mk_dsem("jl1")]
    jit_issued = [0]
    slot_store_sem = [mk_dsem(f"sst{i}") for i in range(3)]
    wd_store_sem = [[mk_dsem(f"wst{f}_{c}") for c in range(11)] for f in range(2)]

    def jit_issue_load(j):
        if j >= len(order0) or j < jit_issued[0]:
            return
        assert j == jit_issued[0]
        s_ = j % 2
        loads, _, _ = jit_spec(*order0[j])
        for (sv, src) in loads:
            k.dma(SP, jit_sem[s_], lambda h, sv=sv, src=src: h.dma_start(out=sv(stage_ap[s_]), in_=src), [], stage_res[s_])
        jit_issued[0] += 1

    def jit_materialize(item, dest, dest_res, store_sem):
        j = jit_pos[item]
        jit_issue_load(j)
        jit_issue_load(j + 1)
        s_ = j % 2
        _, casts, stores = jit_spec(*item)
        for (ov, iv) in casts:
            k.op(DVE, lambda h, ov=ov, iv=iv: h.tensor_copy(out=ov(dest), in_=iv(stage_ap[s_])), stage_res[s_], [dest_res])
        jit_issue_load(j + 2)
        for (dst, ov, r_) in stores:
            k.dma(SP, store_sem, lambda h, dst=dst, ov=ov: h.dma_start(out=dst, in_=ov(dest)), [dest_res], [r_])

    def ensure_loaded(upto):
        while piece_loaded[0] <= min(upto, len(pieces) - 1):
            i = piece_loaded[0]
            piece_loaded[0] += 1
            if i < NP0:
                kind, f_, p_ = tile0_items[i]
                if kind in ("qbs", "kb4"):
                    continue
                s = i % 3
                for (view, src32) in cast_srcs(kind, f_, p_):
                    k.dma(POOL, ring_sem_sw[s], lambda h, s=s, view=view, src32=src32: h.dma_start(out=view(ring[s]), in_=src32),
                          [], [ring_r[s]])
                src, view, r_ = pieces[i]
                k.dma(SP, slot_store_sem[s], lambda h, s=s, src=src, view=view: h.dma_start(out=src, in_=view(ring[s])),
                      [ring_r[s]], [r_])
                continue
            s = i % 3
            src, view, r_ = pieces[i]
            k.dma(SP, ring_sem[s], lambda h, s=s, src=src, view=view: h.dma_start(out=view(ring[s]), in_=src), [r_], [ring_r[s]])

    tile0_items = tile_piece_list()

    def cast_srcs(kind, f, p):
        if kind == "gu":
            lo, hi = p * 256, (p + 1) * 256
            return [(lambda tns: tns[:, 0:2048].rearrange("p (a b) -> p a b", a=8), wgv[f][:, :, lo:hi]),
                    (lambda tns: tns[:, 2048:4096].rearrange("p (a b) -> p a b", a=8), wuv[f][:, :, lo:hi])]
        if kind in ("qa", "ka", "qb", "va"):
            c_src = {"qa": 0, "ka": 512, "qb": 1536, "va": 1024}[kind]
            return [(v3(8, 512), winv[:, :, c_src:c_src + 512])]
        if kind == "vb":
            return [(v3(8, 128), winv[:, :, 2176:2304])]
        if kind == "wo":
            return [(v3(8, 512), woutv[:, :, p * 512:(p + 1) * 512])]
        raise ValueError(kind)

    def take_piece(la=2):
        i = piece_used[0]
        piece_used[0] += 1
        if i < NP0 and tile0_items[i][0] in ("qbs", "kb4"):
            jit_materialize(tile0_items[i], ring[i % 3], ring_r[i % 3], slot_store_sem[i % 3])
        ensure_loaded(i + la)
        return i % 3

    stat_i = [0]
    rot_i = [0]

    def acc_col():
        return stat_col()

    def stat_col():
        j = rot_i[0] % NROT
        rot_i[0] += 1
        return stats[:, NACC + j:NACC + j + 1], rot_r[j]

    def rstd_from(ss_ap, ss_r):
        rs_ap, rs_r = stat_col()
        k.op(ACT, lambda h: h.activation(out=rs_ap, in_=ss_ap, func=AF.Sqrt, bias=epsb[:, :], scale=1.0), [ss_r, const_r], [rs_r])
        rd_ap, rd_r = stat_col()
        k.op(DVE, lambda h: h.reciprocal(out=rd_ap, in_=rs_ap), [rs_r], [rd_r])
        return rd_ap, rd_r

    xn_rot = [0]

    def prenorm_stats(hb, tb, scale_eng=None):
        i = xn_rot[0] % 2
        xn_rot[0] += 1
        ss_ap, ss_r = acc_col()
        src = h_t[hb][:, tb, :]
        k.op(ACT, lambda h: h.activation(out=xn_tok[i][:, :], in_=src, func=AF.Square, scale=1.0 / 32.0, accum_out=ss_ap),
             [h_r[hb][tb], const_r], [xn_tok_r[i], ss_r])
        rd_ap, rd_r = rstd_from(ss_ap, ss_r)
        if scale_eng is POOL:
            k.op(POOL, lambda h: h.tensor_scalar(out=xn_tok[i][:, :], in0=src, scalar1=rd_ap, scalar2=None, op0=ALU.mult),
                 [h_r[hb][tb], rd_r], [xn_tok_r[i]])
        else:
            k.op(ACT, lambda h: h.activation(out=xn_tok[i][:, :], in_=src, func=AF.Copy, scale=rd_ap),
                 [h_r[hb][tb], rd_r], [xn_tok_r[i]])
        return i

    def prenorm_T(i, tb, gi):
        b = next_bank("tr")
        pst = ps[b].bitcast(BF16)

        def f(h):
            ins = None
            for dc in range(NDC):
                ins = h.transpose(pst[:, dc * 128:(dc + 1) * 128], xn_tok[i][:, dc * 128:(dc + 1) * 128], ident[:, :])
            return ins
        k.op(PE, f, [xn_tok_r[i], const_r], [ps_res[b]])
        in0 = pst[:, :].rearrange("p (a b) -> p a b", a=NDC)
        in1 = bass.AP(gpre, gi * 8, [[24, 128], [1, 8], [0, 128]])
        k.op(DVE, lambda h: h.tensor_tensor(out=xnT[:, :, tb * 128:(tb + 1) * 128], in0=in0, in1=in1, op=ALU.mult),
             [ps_res[b], gpre_r], [xnT_r[tb]])

    def mm_group(b, out_ap, pairs, reads, start=True):
        def f(h):
            ins = None
            n = len(pairs)
            for j, (l, r) in enumerate(pairs):
                ins = h.matmul(out_ap, l, r, start=(start and j == 0), stop=(j == n - 1))
            return ins
        return k.op(PE, f, reads, [ps_res[b]])

    def postnorm(hb, tb, banks, gp):
        i = tb % 2
        b0, b1 = banks
        assert b1 == b0 + 1
        st_ap, st_r = acc_col()
        fv = ps_all[:, b0 * 512:(b0 + 2) * 512]
        k.op(ACT, lambda h: h.activation(out=junk2, in_=fv, func=AF.Square, scale=1.0 / 32.0, accum_out=st_ap),
             [ps_res[b0], ps_res[b1], const_r], [junk_r, st_r])
        k.op(DVE, lambda h: h.tensor_tensor(out=tmpA[i][:, :], in0=fv, in1=gpost[gp][:, :], op=ALU.mult),
             [ps_res[b0], ps_res[b1], gp_r[gp]], [tmpA_r[i]])
        rd_ap, rd_r = rstd_from(st_ap, st_r)
        dst = h_t[hb][:, tb, :]
        k.op(DVE, lambda h: h.scalar_tensor_tensor(out=dst, in0=tmpA[i][:, :], scalar=rd_ap, in1=dst, op0=ALU.mult, op1=ALU.add),
             [tmpA_r[i], rd_r, h_r[hb][tb]], [h_r[hb][tb]])

    gu_state = {}

    def gu_first_partial(wd_f, jit):
        s = take_piece()
        load_wd(wd_f, 0, first_tile=jit)
        gv = ring[s][:, :].rearrange("p (g a b) -> p g a b", g=2, a=8)
        if rot["mm"] % 2:
            rot["mm"] += 1
        banks = [next_bank("mm") for _ in range(4)]
        n = 0
        for j in range(2):
            for g_ in range(2):
                b = banks[n]
                n += 1
                mm_group(b, ps[b][:, 0:384], [(gv[:, g_, dc, j * 128:(j + 1) * 128], xnT[:, dc, 0:384]) for dc in range(NDC)],
                         [ring_r[s]] + xnT_r[0:3])
        gu_state["s"], gu_state["banks"] = s, banks

    def gu_first_rest():
        s, banks = gu_state["s"], gu_state["banks"]
        gv = ring[s][:, :].rearrange("p (g a b) -> p g a b", g=2, a=8)
        n = 0
        for j in range(2):
            bg, bu = banks[n], banks[n + 1]
            for g_, b in ((0, bg), (1, bu)):
                mm_group(b, ps[b][:, 384:512], [(gv[:, g_, dc, j * 128:(j + 1) * 128], xnT[:, dc, 384:512]) for dc in range(NDC)],
                         [ring_r[s], xnT_r[3]])
            n += 2
            fc = j
            i = fc % 2
            k.op(ACT, lambda h, bg=bg, i=i: h.activation(out=tmpA[i][:, 0:512], in_=ps[bg][:, :], func=AF.Silu),
                 [ps_res[bg]], [tmpA_r[i]])
            k.op(DVE, lambda h, bu=bu, i=i, fc=fc: h.tensor_tensor(out=act_v[:, fc, :], in0=ps[bu][:, :], in1=tmpA[i][:, 0:512],
                                                                   op=ALU.mult), [ps_res[bu], tmpA_r[i]], [U_r[fc]])

    def ffn_gate_up(wd_f, mid_hook=None, jit=False, hook_every=False, skip_first=False):
        for p in range(11):
            if skip_first and p == 0:
                continue
            s = take_piece()
            if jit and OPT_FINE_WD:
                load_wd(wd_f, p, first_tile=True)
            elif p in (0, 2, 4, 6):
                load_wd(wd_f, p // 2, first_tile=jit)
            if mid_hook is not None and hook_every:
                mid_hook(p)
            elif mid_hook is not None and p == 6:
                mid_hook()
            gv = ring[s][:, :].rearrange("p (g a b) -> p g a b", g=2, a=8)
            for j in range(2):
                fc = 2 * p + j
                bg, bu = next_bank("mm"), next_bank("mm")
                mm_group(bg, ps[bg][:, :], [(gv[:, 0, dc, j * 128:(j + 1) * 128], xnT[:, dc, :]) for dc in range(NDC)],
                         [ring_r[s]] + xnT_r)
                mm_group(bu, ps[bu][:, :], [(gv[:, 1, dc, j * 128:(j + 1) * 128], xnT[:, dc, :]) for dc in range(NDC)],
                         [ring_r[s]] + xnT_r)
                i = fc % 2
                k.op(ACT, lambda h, bg=bg, i=i: h.activation(out=tmpA[i][:, 0:512], in_=ps[bg][:, :], func=AF.Silu),
                     [ps_res[bg]], [tmpA_r[i]])
                k.op(DVE, lambda h, bu=bu, i=i, fc=fc: h.tensor_tensor(out=act_v[:, fc, :], in0=ps[bu][:, :], in1=tmpA[i][:, 0:512],
                                                                       op=ALU.mult), [ps_res[bu], tmpA_r[i]], [U_r[fc]])

    def load_wd(f, q, first_tile=False):
        if first_tile and OPT_FINE_WD:
            f0, f1 = 2 * q, 2 * q + 2
        else:
            f0, f1 = q * 6, min((q + 1) * 6, NFC)
        c0, c1 = f0 * D, f1 * D
        if first_tile:
            dstv = Wd[:, c0:c1].rearrange("p (a b) -> p a b", a=f1 - f0)
            k.dma(POOL, wd_sem_sw, lambda h: h.dma_start(out=dstv, in_=wdv[f][:, f0:f1, :]), [], [Wd_r])
            k.dma(SP, wd_store_sem[f][q], lambda h: h.dma_start(out=wd_s[f][:, c0:c1], in_=Wd[:, c0:c1]), [Wd_r], [wd_scr_r[f][q]])
        else:
            k.dma(SP, wd_sem, lambda h: h.dma_start(out=Wd[:, c0:c1], in_=wd_s[f][:, c0:c1]), wd_scr_r[f], [Wd_r])

    def ffn_down(tb):
        banks = []
        if rot["mm"] % 2:
            rot["mm"] += 1
        for hf in range(2):
            b = next_bank("mm")
            mm_group(b, ps[b][:, :], [(act_v[:, fc, tb * 128:(tb + 1) * 128], Wd[:, fc * D + hf * 512: fc * D + (hf + 1) * 512])
                                      for fc in range(NFC)], [Wd_r] + U_r)
            banks.append(b)
        return banks

    def rope_tables():
        ang, kq = rsc[:, 0:512], rsc[:, 512:1024]
        invf, sgn = cst[:, 0:1], cst[:, 1:2]
        rr, cs = [rsc_r], [CS_r]
        k.op(DVE, lambda h: h.tensor_copy(out=kq, in_=posi), [posi_r], rr)
        k.op(DVE, lambda h: h.tensor_scalar(out=ang, in0=kq, scalar1=invf, scalar2=None, op0=ALU.mult), rr + [cst_r], rr)
        yield
        k.op(DVE, lambda h: h.tensor_scalar(out=St[:, :], in0=ang, scalar1=float(1.0 / TWO_PI), scalar2=None, op0=ALU.mult), rr + cs, cs)
        k.op(DVE, lambda h: h.tensor_copy(out=kq.bitcast(I32), in_=St[:, :]), rr + cs, rr)
        k.op(DVE, lambda h: h.tensor_copy(out=St[:, :], in_=kq.bitcast(I32)), rr + cs, cs)
        yield
        k.op(DVE, lambda h: h.scalar_tensor_tensor(out=Ct[:, :], in0=St[:, :], scalar=-CW1, in1=ang, op0=ALU.mult, op1=ALU.add), rr + cs, cs)
        k.op(DVE, lambda h: h.scalar_tensor_tensor(out=Ct[:, :], in0=St[:, :], scalar=-CW2, in1=Ct[:, :], op0=ALU.mult, op1=ALU.add), cs, cs)
        yield
        k.op(DVE, lambda h: h.tensor_scalar(out=Ct[:, :], in0=Ct[:, :], scalar1=-PI_LO, scalar2=PI_LO, op0=ALU.max, op1=ALU.min), cs, cs)
        k.op(ACT, lambda h: h.activation(out=St[:, :], in_=Ct[:, :], func=AF.Sin, scale=sgn), cs + [cst_r], cs)
        yield
        k.op(DVE, lambda h: h.tensor_scalar(out=ang, in0=ang, scalar1=float(0.5 * np.pi), scalar2=None, op0=ALU.add), rr, rr)
        k.op(DVE, lambda h: h.tensor_scalar(out=Ct[:, :], in0=ang, scalar1=float(1.0 / TWO_PI), scalar2=None, op0=ALU.mult), rr + cs, cs)
        yield
        k.op(DVE, lambda h: h.tensor_copy(out=kq.bitcast(I32), in_=Ct[:, :]), rr + cs, rr)
        k.op(DVE, lambda h: h.tensor_copy(out=kq, in_=kq.bitcast(I32)), rr, rr)
        yield
        k.op(DVE, lambda h: h.scalar_tensor_tensor(out=Ct[:, :], in0=kq, scalar=-CW1, in1=ang, op0=ALU.mult, op1=ALU.add), rr + cs, cs)
        k.op(DVE, lambda h: h.scalar_tensor_tensor(out=Ct[:, :], in0=kq, scalar=-CW2, in1=Ct[:, :], op0=ALU.mult, op1=ALU.add), rr + cs, cs)
        yield
        k.op(DVE, lambda h: h.tensor_scalar(out=Ct[:, :], in0=Ct[:, :], scalar1=-PI_LO, scalar2=PI_LO, op0=ALU.max, op1=ALU.min), cs, cs)
        k.op(ACT, lambda h: h.activation(out=Ct[:, :], in_=Ct[:, :], func=AF.Sin), cs, cs)
        yield

    def proj_block(s, width, blk):
        v = ring[s][:, 0:8 * width].rearrange("p (a b) -> p a b", a=8)
        b = next_bank("mm")
        mm_group(b, ps[b][:, :], [(v[:, dc, blk * 128:(blk + 1) * 128], xnT[:, dc, :]) for dc in range(NDC)], [ring_r[s]] + xnT_r)
        return b

    def rope_apply(bx, by, dst_ap, dst_r, i):
        k.op(DVE, lambda h: h.tensor_tensor(out=tmpA[i][:, 0:512], in0=ps[bx][:, :], in1=Ct[:, :], op=ALU.mult),
             [ps_res[bx], CS_r], [tmpA_r[i]])
        k.op(DVE, lambda h: h.tensor_tensor(out=tmpA[i][:, 512:1024], in0=ps[by][:, :], in1=St[:, :], op=ALU.mult),
             [ps_res[by], CS_r, tmpA_r[i]], [tmpA_r[i]])
        k.op(POOL, lambda h: h.tensor_tensor(out=dst_ap, in0=tmpA[i][:, 0:512], in1=tmpA[i][:, 512:1024], op=ALU.add),
             [tmpA_r[i]], [dst_r])

    def mixer_proj(cur):
        s = take_piece()
        for blk in range(4):
            b = proj_block(s, 512, blk)
            k.op(ACT, lambda h, b=b, blk=blk: h.copy(out=QaT[:, blk, :], in_=ps[b][:, :]), [ps_res[b]], [QaT_r[blk]])
        s = take_piece()
        for blk in range(4):
            b = proj_block(s, 512, blk)
            k.op(DVE, lambda h, b=b, blk=blk: h.tensor_copy(out=KaT[cur][:, blk, :], in_=ps[b][:, :]), [ps_res[b]], [KaT_r[cur][blk]])
        s1 = take_piece()
        s2 = take_piece(1)
        for blk in range(4):
            bx = proj_block(s1, 512, blk)
            by = proj_block(s2, 512, blk)
            rope_apply(bx, by, QbT[:, blk, :], QbT_r[blk], blk % 2)
        s = take_piece()
        for v in range(2):
            bx = proj_block(s, 512, 2 * v)
            by = proj_block(s, 512, 2 * v + 1)
            rope_apply(bx, by, KbT[cur][:, v, :], KbT_r[cur][v], v)

    vslots = {}

    def v_take():
        vslots["a"] = take_piece()
        vslots["b"] = take_piece(1)

    def v_proj(cur, tb):
        sva, svb = vslots["a"], vslots["b"]
        vva = ring[sva][:, 0:4096].rearrange("p (a b) -> p a b", a=8)
        vvb = ring[svb][:, 0:1024].rearrange("p (a b) -> p a b", a=8)
        ba = next_bank("mm")
        mm_group(ba, ps[ba][:, :], [(xnT[:, dc, tb * 128:(tb + 1) * 128], vva[:, dc, :]) for dc in range(NDC)],
                 [ring_r[sva], xnT_r[tb]])
        bb = next_bank("mm")
        mm_group(bb, ps[bb][:, 0:128], [(xnT[:, dc, tb * 128:(tb + 1) * 128], vvb[:, dc, :]) for dc in range(NDC)],
                 [ring_r[svb], xnT_r[tb]])
        src = ps[ba][:, :].rearrange("p (h two e) -> p h two e", h=4, two=2)
        dstv = Va[cur][:, tb, :].rearrange("p (h four e) -> p h four e", h=4, four=4)
        k.op(ACT, lambda h: h.copy(out=dstv[:, :, 0, :], in_=src[:, :, 0, :]), [ps_res[ba]], [Va_r[cur][tb]])
        k.op(DVE, lambda h: h.tensor_copy(out=dstv[:, :, 3, :], in_=src[:, :, 1, :]), [ps_res[ba]], [Va_r[cur][tb]])
        srcb = ps[bb][:, 0:128].rearrange("p (g e) -> p g e", g=2)
        dstb = Vb[cur][:, tb, :].rearrange("p (g three e) -> p g three e", g=2, three=3)[:, :, 1, :]
        k.op(DVE, lambda h: h.tensor_copy(out=dstb, in_=srcb), [ps_res[bb]], [Vb_r[cur][tb]])

    pt_rot = [0]

    def attention(cur, prev, first):
        steps = []
        for hd in range(8):
            hbk, pb = hd // 2, (hd % 2) * 64
            kts = [4, 5, 6, 7] if first else [3, 0, 1, 2, 4, 5, 6, 7]
            for n, kt in enumerate(kts):
                buf = cur if kt >= 4 else prev
                tbk = kt % 4
                qlo, qhi = max(0, 2 * kt - 8), min(7, 2 * kt + 1)
                extra = []
                if kt >= 3:
                    ilo, ihi = max(0, 2 * kt - 8), min(7, 2 * kt - 5)
                    extra.append((BT[:, hd, (ilo - (2 * kt - 8)) * 64:(ihi - (2 * kt - 8) + 1) * 64], (ilo - qlo) * 64, (ihi - qlo + 1) * 64))
                zero = []
                if 2 * kt + 1 <= 7:
                    i_ = 2 * kt + 1
                    zero.append((slice(0, 64), (i_ - qlo) * 64))
                steps.append(dict(zero=zero,
                    K=KaT[buf][pb:pb + 64, hbk, tbk * 128:(tbk + 1) * 128], K_r=KaT_r[buf][hbk],
                    Q=QaT[pb:pb + 64, hbk, qlo * 64:(qhi + 1) * 64], Q_r=QaT_r[hbk],
                    V=Va[buf][:, tbk, hd * 128:(hd + 1) * 128], V_r=Va_r[buf][tbk],
                    q0=qlo * 64, ncols=(qhi + 1 - qlo) * 64, extra=extra, head=("a", hd), first=(n == 0), last=(n == len(kts) - 1),
                    odd=hd % 2, ot=OT[pb:pb + 64, hbk, :], ot_r=OT_r[hbk], sink=None))
        for hd in range(8):
            hbk, pb, g = hd // 2, (hd % 2) * 64, hd // 4
            var = 0 if pb == g * 64 else 1
            kts = [4, 5, 6, 7] if first else [3, 4, 5, 6, 7]
            for n, kt in enumerate(kts):
                buf = cur if kt >= 4 else prev
                tbk = kt % 4
                ilo, ihi = max(0, 2 * kt - 8), min(7, 2 * kt - 5)
                extra = []
                zero = []
                if 2 * kt - 8 >= 0:
                    zero.append((slice(64, 128), 0))
                if 2 * kt - 5 <= 7:
                    zero.append((slice(0, 64), (2 * kt - 5 - ilo) * 64))
                voff = g * 192 + (64 if hd % 2 == 0 else 0)
                steps.append(dict(zero=zero,
                    K=KbT[buf][pb:pb + 64, var, tbk * 128:(tbk + 1) * 128], K_r=KbT_r[buf][var],
                    Q=QbT[pb:pb + 64, hbk, ilo * 64:(ihi + 1) * 64], Q_r=QbT_r[hbk],
                    V=Vb[buf][:, tbk, voff:voff + 128], V_r=Vb_r[buf][tbk],
                    q0=ilo * 64, ncols=(ihi + 1 - ilo) * 64, extra=extra, head=("b", hd), first=(n == 0), last=(n == len(kts) - 1),
                    odd=hd % 2, ot=OT[pb:pb + 64, 4 + hbk, :], ot_r=OT_r[4 + hbk], sink=hd))

        def interleave(lst):
            out = []
            by_head = {}
            order = []
            for sp in lst:
                if sp["head"] not in by_head:
                    by_head[sp["head"]] = []
                    order.append(sp["head"])
                by_head[sp["head"]].append(sp)
            for j in range(0, len(order), 2):
                a, b = by_head[order[j]], by_head[order[j + 1]]
                for x, y in zip(a, b):
                    out += [x, y]
            return out
        steps = interleave(steps)
        cur_o = {}

        pair_rot = [0]

        def emit_st_pair(spa, spb):
            bA = ST_BANKS[(pair_rot[0] % 2) * 2]
            piA = (pair_rot[0] % 3) * 2
            pair_rot[0] += 1
            n = spa["ncols"]
            assert spb["ncols"] == n
            for sp, b, pi in ((spa, bA, piA), (spb, bA + 1, piA + 1)):
                sp["stb"], sp["pt"] = b, pi
                k.op(PE, lambda h, sp=sp, b=b: h.matmul(ps[b][:, 0:n], sp["K"], sp["Q"], start=True, stop=True),
                     [sp["K_r"], sp["Q_r"]], [ps_res[b]])
            outv = U[:, 8192 + 512 * piA: 8192 + 512 * (piA + 2)].rearrange("p (two n) -> p two n", two=2)[:, :, 0:n]
            k.op(ACT, lambda h: h.activation(out=outv, in_=pair_ap(bA, n), func=AF.Exp, scale=0.125),
                 [ps_res[bA], ps_res[bA + 1]], [PT_r[piA], PT_r[piA + 1]])
            for sp in (spa, spb):
                pi = sp["pt"]
                for (tab, c0, c1) in sp["extra"]:
                    k.op(POOL, lambda h, tab=tab, c0=c0, c1=c1, pi=pi: h.tensor_tensor(out=PT[pi][:, c0:c1], in0=PT[pi][:, c0:c1], in1=tab, op=ALU.mult),
                         [PT_r[pi], const_r], [PT_r[pi]])
                for (rows, c0) in sp["zero"]:
                    k.op(POOL, lambda h, rows=rows, c0=c0, pi=pi: h.memset(PT[pi][rows, c0:c0 + 64], 0.0), [PT_r[pi]], [PT_r[pi]])

        def emit_pv(sp):
            if sp["first"]:
                cur_o[sp["head"]] = next_bank("o")
            ob = cur_o[sp["head"]]
            pi = sp["pt"]
            k.op(PE, lambda h: h.matmul(ps[ob][:, sp["q0"]:sp["q0"] + sp["ncols"]], sp["V"], PT[pi][:, 0:sp["ncols"]],
                                        start=sp["first"], stop=sp["last"], skip_group_check=True),
                 [sp["V_r"], PT_r[pi]], [ps_res[ob]])
            if sp["last"]:
                den = slice(0, 64) if sp["odd"] else slice(64, 128)
                dat = slice(64, 128) if sp["odd"] else slice(0, 64)
                i = sp["head"][1] % 2
                rec = tmpA[i][den, 0:512]
                use_act = (sp["sink"] is not None) or (OPT_BAL_RECIP and sp["head"][1] % 2 == 1)
                if sp["sink"] is not None:
                    hd = sp["sink"]
                    if use_act:
                        k.op(ACT, lambda h: h.activation(out=rec, in_=ps[ob][den, :], func=AF.Ln, bias=esink[den, hd:hd + 1], scale=1.0),
                             [ps_res[ob], sk_r], [tmpA_r[i]])
                        k.op(ACT, lambda h: h.activation(out=rec, in_=rec, func=AF.Exp, scale=-1.0), [tmpA_r[i]], [tmpA_r[i]])
                    else:
                        k.op(DVE, lambda h: h.tensor_scalar(out=rec, in0=ps[ob][den, :], scalar1=esink[den, hd:hd + 1], scalar2=None,
                                                            op0=ALU.add), [ps_res[ob], sk_r], [tmpA_r[i]])
                        k.op(DVE, lambda h: h.reciprocal(out=rec, in_=rec), [tmpA_r[i]], [tmpA_r[i]])
                elif use_act:
                    k.op(ACT, lambda h: h.activation(out=rec, in_=ps[ob][den, :], func=AF.Ln), [ps_res[ob]], [tmpA_r[i]])
                    k.op(ACT, lambda h: h.activation(out=rec, in_=rec, func=AF.Exp, scale=-1.0), [tmpA_r[i]], [tmpA_r[i]])
                else:
                    k.op(DVE, lambda h: h.reciprocal(out=rec, in_=ps[ob][den, :]), [ps_res[ob]], [tmpA_r[i]])
                k.op(DVE, lambda h: h.tensor_tensor(out=sp["ot"], in0=ps[ob][dat, :], in1=rec, op=ALU.mult),
                     [ps_res[ob], tmpA_r[i]], [sp["ot_r"]])

        LAG = 4
        for n in range(0, len(steps), 2):
            emit_st_pair(steps[n], steps[n + 1])
            if n >= LAG:
                emit_pv(steps[n - LAG])
                emit_pv(steps[n - LAG + 1])
        for sp in steps[len(steps) - LAG:]:
            emit_pv(sp)

    def reset_mm8():
        rot["mm8"] = 0

    def wout_tb(tb, s0, s1):
        banks = []
        for hf, s in enumerate((s0, s1)):
            v = ring[s][:, 0:4096].rearrange("p (a b) -> p a b", a=8)
            b = next_bank("mm8")
            mm_group(b, ps[b][:, :], [(OT[:, c, tb * 128:(tb + 1) * 128], v[:, c, :]) for c in range(8)], [ring_r[s]] + OT_r)
            banks.append(b)
        return banks

    def final_norm_store(hb, tb, seq, s0):
        ss_ap, ss_r = acc_col()
        src = h_t[hb][:, tb, :]
        k.op(ACT, lambda h: h.activation(out=tmpA[tb % 2][:, :], in_=src, func=AF.Square, scale=1.0 / 32.0, accum_out=ss_ap),
             [h_r[hb][tb], const_r], [tmpA_r[tb % 2], ss_r])
        rd_ap, rd_r = rstd_from(ss_ap, ss_r)
        k.op(DVE, lambda h: h.scalar_tensor_tensor(out=src, in0=src, scalar=rd_ap, in1=gpost[3][:, :], op0=ALU.mult, op1=ALU.mult),
             [h_r[hb][tb], rd_r, gp_r[3]], [h_r[hb][tb]])
        dst = out_d[seq, s0 + tb * 128: s0 + (tb + 1) * 128, :]
        k.dma(SP, out_sem[hb], lambda h: h.dma_start(out=dst, in_=src), [h_r[hb][tb]], [])

    def load_x(t):
        hb = t % 2
        seq, s0 = t // TILES_PER_SEQ, (t % TILES_PER_SEQ) * T
        srcv = x_d[seq, s0:s0 + T, :].rearrange("(tb p) d -> p tb d", p=128)
        k.dma(SP, x_sem[hb], lambda h: h.dma_start(out=h_t[hb][:, :, :], in_=srcv), [], h_r[hb])

    def load_pos(t):
        seq, s0 = t // TILES_PER_SEQ, (t % TILES_PER_SEQ) * T
        k.dma(SP, pos_sem, lambda h: h.dma_start(out=posi, in_=bass.AP(pos_d, seq * SEQ + s0, [[0, 128], [1, T]])), [], [posi_r])

    def after_tile0_jit():
        for r_ in Va_r[1] + KaT_r[1] + KbT_r[1]:
            r_.w = kv1_r.w
            r_.r = dict(kv1_r.r)
        k.op(POOL, lambda h: h.memset(Va[1][:, :, :], 1.0), [], Va_r[1])

    load_x(0)
    load_pos(0)
    if n_tiles > 1:
        pass
    try:
        _tile_loop(locals())
    except StopBuild:
        dstv = out_d[0, 0:T, :].rearrange("(tb p) d -> p tb d", p=128)
        k.dma(SP, out_sem[0], lambda h: h.dma_start(out=dstv, in_=h_t[0][:, :, :]), h_r[0], [])
        flat1 = h_t[1][:, :, :].rearrange("p a b -> p (a b)")
        k.op(DVE, lambda h: h.tensor_copy(out=flat1, in_=U[:, 4096:8192]), OT_r, h_r[1])
        k.dma(SP, out_sem[0], lambda h: h.dma_start(out=out_d[1, 0:T, :].rearrange("(p a) d -> p (a d)", a=4), in_=flat1), h_r[1], [])
        k.op(DVE, lambda h: h.tensor_copy(out=flat1, in_=U[:, 0:4096]), QaT_r + QbT_r, h_r[1])
        k.dma(SP, out_sem[0], lambda h: h.dma_start(out=out_d[2, 0:T, :].rearrange("(p a) d -> p (a d)", a=4), in_=flat1), h_r[1], [])
        k.op(DVE, lambda h: h.tensor_copy(out=flat1[:, 0:2048], in_=KaT[0][:, :, :].rearrange("p a b -> p (a b)")), KaT_r[0], h_r[1])
        k.op(DVE, lambda h: h.tensor_copy(out=flat1[:, 2048:3072], in_=KbT[0][:, :, :].rearrange("p a b -> p (a b)")), KbT_r[0], h_r[1])
        k.op(DVE, lambda h: h.tensor_copy(out=flat1[:, 3072:4096], in_=Va[0][:, 0, :]), Va_r[0], h_r[1])
        k.dma(SP, out_sem[0], lambda h: h.dma_start(out=out_d[3, 0:T, :].rearrange("(p a) d -> p (a d)", a=4), in_=flat1), h_r[1], [])
    k.wait_all(SP, [Ev(d, d.cnt) for d in out_sem if d.cnt > 0])


def _tile_loop(L):
    globals_needed = ("n_tiles checkpoint load_x load_pos prenorm_stats prenorm_T ensure_loaded piece_used load_wd ffn_gate_up "
                      "ffn_down postnorm rope_tables mixer_proj attention take_piece wout_tb final_norm_store v_take v_proj after_tile0_jit reset_mm8 gu_first_partial gu_first_rest").split()
    (n_tiles, checkpoint, load_x, load_pos, prenorm_stats, prenorm_T, ensure_loaded, piece_used, load_wd, ffn_gate_up,
     ffn_down, postnorm, rope_tables, mixer_proj, attention, take_piece, wout_tb, final_norm_store, v_take, v_proj, after_tile0_jit, reset_mm8, gu_first_partial, gu_first_rest) = [L[n] for n in globals_needed]
    for t in range(n_tiles):
        hb = t % 2
        cur, prev = t % 2, (t - 1) % 2
        seq, s0 = t // TILES_PER_SEQ, (t % TILES_PER_SEQ) * T
        first = (t % TILES_PER_SEQ == 0)
        has_next = t + 1 < n_tiles
        if t == 0:
            for tb in range(4):
                i = prenorm_stats(hb, tb)
                prenorm_T(i, tb, 0)
        checkpoint("pre1")
        ensure_loaded(piece_used[0] + 1)

        rgen = rope_tables()

        def hook1(p):
            if p >= 2:
                try:
                    next(rgen)
                except StopIteration:
                    pass
            if p == 10:
                for _ in rgen:
                    pass
                if has_next:
                    load_pos(t + 1)
        ffn_gate_up(0, hook1, jit=(t == 0), hook_every=True)
        checkpoint("gu1")
        if has_next and t >= 1:
            load_x(t + 1)
        sl = {}
        for tb in range(4):
            banks = ffn_down(tb)
            postnorm(hb, tb, banks, 0)
            if tb < 2:
                sl[tb] = prenorm_stats(hb, tb)
            if tb == 2:
                prenorm_T(sl[0], 0, 1)
                sl[2] = prenorm_stats(hb, 2)
        prenorm_T(sl[1], 1, 1)
        sl[3] = prenorm_stats(hb, 3)
        prenorm_T(sl[2], 2, 1)
        v_take()
        for tb in range(3):
            v_proj(cur, tb)
        prenorm_T(sl[3], 3, 1)
        v_proj(cur, 3)
        checkpoint("ffn1")
        mixer_proj(cur)
        if t == 0 and has_next:
            after_tile0_jit()
            load_x(1)
        checkpoint("proj")
        attention(cur, prev, first)
        checkpoint("attn")
        s0w = take_piece()
        s1w = take_piece(1)
        reset_mm8()
        sl = {}
        PO = None
        for tb in range(4):
            banks = wout_tb(tb, s0w, s1w)
            postnorm(hb, tb, banks, 1)
            if tb < 2:
                sl[tb] = prenorm_stats(hb, tb, PO)
        prenorm_T(sl[0], 0, 2)
        sl[2] = prenorm_stats(hb, 2, PO)
        prenorm_T(sl[1], 1, 2)
        sl[3] = prenorm_stats(hb, 3, PO)
        prenorm_T(sl[2], 2, 2)
        if OPT_SPLIT_GU:
            gu_first_partial(1, t == 0)
        prenorm_T(sl[3], 3, 2)
        if OPT_SPLIT_GU:
            gu_first_rest()
        checkpoint("wout")
        nxt = []

        hoist = has_next

        def hook():
            if hoist:
                nxt.append(prenorm_stats((t + 1) % 2, 0))
                nxt.append(prenorm_stats((t + 1) % 2, 1))
        ffn_gate_up(1, hook, jit=(t == 0), skip_first=OPT_SPLIT_GU)
        if hoist:
            prenorm_T(nxt[0], 0, 0)
            prenorm_T(nxt[1], 1, 0)
            nxt.append(prenorm_stats((t + 1) % 2, 2))
            nxt.append(prenorm_stats((t + 1) % 2, 3))
        for tb in range(4):
            banks = ffn_down(tb)
            if hoist and tb < 2:
                prenorm_T(nxt[2 + tb], 2 + tb, 0)
            postnorm(hb, tb, banks, 2)
            final_norm_store(hb, tb, seq, s0)


def make_consts():
    c = np.zeros((128, 264), np.float32)
    c[:, 0:128] = np.eye(128, dtype=np.float32)
    c[:, 128:256] = np.eye(128, dtype=np.float32)[::-1]
    inv_freq = np.power(np.float32(500000.0), -np.arange(8, dtype=np.float32) * np.float32(2.0 / 16.0)).astype(np.float32)
    for p in range(128):
        e = p % 64
        if e < 16:
            c[p, 256] = inv_freq[e % 8]
        c[p, 257] = -1.0 if e < 8 else 1.0
    return c


def kernel(x, positions, ffn1_pre_g, ffn1_w_gate, ffn1_w_up, ffn1_w_down, ffn1_post_g,
           mix_pre_g, w_in, rel_bias_a, sinks_b, w_out, mix_post_g,
           ffn2_pre_g, ffn2_w_gate, ffn2_w_up, ffn2_w_down, ffn2_post_g, final_g):
    f32 = np.float32
    x = np.ascontiguousarray(np.asarray(x, f32))
    positions = np.ascontiguousarray(np.asarray(positions, np.int32))
    gains = np.ascontiguousarray(np.stack([np.asarray(g, f32)[0] for g in (
        ffn1_pre_g, ffn1_post_g, mix_pre_g, mix_post_g, ffn2_pre_g, ffn2_post_g, final_g)], 0))
    shared = {
        "wg0": np.ascontiguousarray(np.asarray(ffn1_w_gate, f32)[0]), "wu0": np.ascontiguousarray(np.asarray(ffn1_w_up, f32)[0]),
        "wd0": np.ascontiguousarray(np.asarray(ffn1_w_down, f32)[0]),
        "wg1": np.ascontiguousarray(np.asarray(ffn2_w_gate, f32)[0]), "wu1": np.ascontiguousarray(np.asarray(ffn2_w_up, f32)[0]),
        "wd1": np.ascontiguousarray(np.asarray(ffn2_w_down, f32)[0]),
        "win": np.ascontiguousarray(np.asarray(w_in, f32)[0]), "wout": np.ascontiguousarray(np.asarray(w_out, f32)[0]),
        "gains": gains, "relb": np.ascontiguousarray(np.asarray(rel_bias_a, f32)[0]),
        "sinks": np.ascontiguousarray(np.asarray(sinks_b, f32)), "consts": make_consts(),
    }
    nc = build()
    in_maps = []
    for c in range(NCORES):
        m = dict(shared)
        m["x"] = x[c * SEQ_PER_CORE:(c + 1) * SEQ_PER_CORE]
        m["pos"] = positions[c * SEQ_PER_CORE:(c + 1) * SEQ_PER_CORE]
        in_maps.append(m)
    res = run_bass_kernel_spmd(nc, in_maps, core_ids=list(range(NCORES)))
    return np.concatenate([r["out"] for r in res.results], axis=0).astype(np.float32)
```

```python
import numpy as np
from contextlib import ExitStack
import concourse.bass as bass
import concourse.mybir as mybir
from concourse.bass_utils import run_bass_kernel_spmd

F32 = mybir.dt.float32
BF16 = mybir.dt.bfloat16
I32 = mybir.dt.int32
AF = mybir.ActivationFunctionType
ALU = mybir.AluOpType

NCORES = 8
SEQ = 2048
D = 1024
DFF = 2816
T = 512
NFC = 22
NDC = 8
SEQ_PER_CORE = 4
TILES_PER_SEQ = SEQ // T
EPS = 1e-6
NEG = -30000.0
WINB_COLS = 3200
C_QA, C_KA, C_QB, C_QBS, C_KB4, C_VA, C_VB = 0, 512, 1024, 1536, 2048, 2560, 3072
TWO_PI = 2.0 * np.pi
CW1 = float(np.float32(6.28125))
CW2 = float(np.float32(TWO_PI - 6.28125))
OPT_FINE_WD = False
OPT_BAL_RECIP = True
OPT_SPLIT_GU = True
PI_LO = 3.1415925


class Ev:
    __slots__ = ("src", "c")

    def __init__(self, src, c):
        self.src = src
        self.c = c


class Res:
    __slots__ = ("name", "w", "r", "excl")

    def __init__(self, name="", excl=False):
        self.name = name
        self.w = None
        self.r = {}
        self.excl = excl


class Eng:
    def __init__(self, name, handle, sem):
        self.name = name
        self.h = handle
        self.sem = sem
        self.cnt = 0
        self.seen = {}
        self.thunks = []


class DSem:
    def __init__(self, name, sem):
        self.name = name
        self.sem = sem
        self.cnt = 0


class K:
    def __init__(self):
        self.engs = []

    def _deps(self, eng, reads, writes):
        need = {}

        def add(ev):
            if ev is None:
                return
            if need.get(ev.src, 0) < ev.c:
                need[ev.src] = ev.c

        for r in reads:
            add(r.w)
        for w in writes:
            add(w.w)
            for ev in w.r.values():
                add(ev)
        for src, c in need.items():
            if eng.seen.get(src, 0) < c:
                eng.seen[src] = c
                eng.h.wait_ge(src.sem, c)

    def op(self, eng, fn, reads=(), writes=()):
        ex = [r for r in reads if r.excl]
        if ex:
            writes = list(writes) + ex
        self._deps(eng, reads, writes)
        eng.cnt += 1
        ev = Ev(eng, eng.cnt)
        fn(eng.h).then_inc(eng.sem, 1)
        for r in reads:
            r.r[eng] = ev
        for w in writes:
            w.w = ev
            w.r = {}
        return ev

    def dma(self, eng, dsem, fn, reads=(), writes=()):
        self._deps(eng, reads, writes)
        dsem.cnt += 16
        ev = Ev(dsem, dsem.cnt)
        fn(eng.h).then_inc(dsem.sem, 16)
        for r in reads:
            r.r[dsem] = ev
        for w in writes:
            w.w = ev
            w.r = {}
        return ev

    def wait_all(self, eng, evs):
        for ev in evs:
            if ev is not None and eng.seen.get(ev.src, 0) < ev.c:
                eng.seen[ev.src] = ev.c
                eng.h.wait_ge(ev.src.sem, ev.c)


class StopBuild(Exception):
    pass


def build(n_tiles=16, stop=None):
    nc = bass.Bass("TRN2", target_bir_lowering=False)
    es = ExitStack()

    def checkpoint(name):
        if stop == name:
            raise StopBuild()
    try:
        _build_body(nc, es, n_tiles, checkpoint)
    except StopBuild:
        pass
    es.close()
    return nc


def _build_body(nc, es, n_tiles, checkpoint):

    def dram(name, shape, dt, kind="Internal"):
        return nc.dram_tensor(name, shape, dt, kind=kind)

    x_d = dram("x", [SEQ_PER_CORE, SEQ, D], F32, "ExternalInput")
    pos_d = dram("pos", [SEQ_PER_CORE, SEQ], I32, "ExternalInput")
    wg_d = [dram(f"wg{i}", [D, DFF], F32, "ExternalInput") for i in range(2)]
    wu_d = [dram(f"wu{i}", [D, DFF], F32, "ExternalInput") for i in range(2)]
    wd_d = [dram(f"wd{i}", [DFF, D], F32, "ExternalInput") for i in range(2)]
    win_d = dram("win", [D, 2304], F32, "ExternalInput")
    wout_d = dram("wout", [D, D], F32, "ExternalInput")
    gains_d = dram("gains", [7, D], F32, "ExternalInput")
    relb_d = dram("relb", [8, 257], F32, "ExternalInput")
    sinks_d = dram("sinks", [1, 8], F32, "ExternalInput")
    consts_d = dram("consts", [128, 264], F32, "ExternalInput")
    out_d = dram("out", [SEQ_PER_CORE, SEQ, D], F32, "ExternalOutput")
    gu_s = [dram(f"gu_s{i}", [11, 128, 4096], BF16) for i in range(2)]
    wd_s = [dram(f"wd_s{i}", [128, NFC * D], BF16) for i in range(2)]
    win_s = dram("win_s", [128, NDC, WINB_COLS], BF16)
    wout_s = dram("wout_s", [128, NDC, D], BF16)
    gt_s = dram("gt_s", [8, 384], F32)

    k = K()
    sem_i = [0]

    def newsem(name):
        sem_i[0] += 1
        return es.enter_context(nc.semaphore(f"{name}_{sem_i[0]}"))

    PE = Eng("pe", nc.tensor, newsem("pe"))
    ACT = Eng("act", nc.scalar, newsem("act"))
    DVE = Eng("dve", nc.vector, newsem("dve"))
    POOL = Eng("pool", nc.gpsimd, newsem("pool"))
    SP = Eng("sp", nc.sync, newsem("sp"))
    engs = [PE, ACT, DVE, POOL, SP]

    def dsem(name):
        return DSem(name, newsem(name))

    def barrier():
        evs = [Ev(e, e.cnt) for e in engs if e.cnt > 0] + [Ev(d, d.cnt) for d in all_dsems if d.cnt > 0]
        for e in engs:
            k.wait_all(e, evs)

    all_dsems = []

    def mk_dsem(name):
        d = dsem(name)
        all_dsems.append(d)
        return d

    ps_all = es.enter_context(nc.psum_tensor("ps_all", [128, 8 * 512], F32))

    class Bank:
        def __init__(self, t, b, w):
            self.t, self.b, self.w = t, b, w

        def __getitem__(self, key):
            p, c = key
            c0 = 0 if c.start is None else c.start
            c1 = self.w if c.stop is None else c.stop
            return self.t[p, self.b * self.w + c0: self.b * self.w + c1]

        def bitcast(self, dt):
            return Bank(self.t.bitcast(dt), self.b, self.w * 2)

    ps = [Bank(ps_all, i, 512) for i in range(8)]
    ps_res = [Res(f"ps{i}", excl=True) for i in range(8)]
    MM_BANKS = [0, 1, 2, 3]
    ST_BANKS = [4, 5, 6, 7]
    TR_BANKS = [4, 5]
    O_BANKS = [0, 1, 2, 3]
    rot = {"mm": 0, "st": 0, "o": 0, "tr": 0, "mm8": 0}

    def pair_ap(b, ncols):
        return ps_all[:, b * 512:(b + 2) * 512].rearrange("p (two n) -> p two n", two=2)[:, :, 0:ncols]

    def next_bank(pool):
        banks = {"mm": MM_BANKS, "st": ST_BANKS, "o": O_BANKS, "tr": TR_BANKS, "mm8": [6, 7, 0, 1, 2, 3]}[pool]
        b = banks[rot[pool] % len(banks)]
        rot[pool] += 1
        return b

    def v3(a, b):
        return lambda tns: tns[:, 0:a * b].rearrange("p (a b) -> p a b", a=a)

    def v3s(a, b, lo, hi):
        return lambda tns: tns[:, 0:a * b].rearrange("p (a b) -> p a b", a=a)[:, :, lo:hi]

    def flat(n):
        return lambda tns: tns[:, 0:n]

    def head_swap_casts(n_heads, src_off, dst_off, W_src, W_dst):
        res = []
        for (d0, s0, n) in ((0, 8, 8), (8, 0, 8), (16, 16, 48)):
            def ov(tns, d0=d0, n=n):
                a = tns[:, 0:8 * W_dst].rearrange("p (a b) -> p a b", a=8)[:, :, dst_off:dst_off + n_heads * 64]
                return a.rearrange("p a (h e) -> p a h e", h=n_heads)[:, :, :, d0:d0 + n]

            def iv(tns, s0=s0, n=n):
                a = tns[:, 0:8 * W_src].rearrange("p (a b) -> p a b", a=8)[:, :, src_off:src_off + n_heads * 64]
                return a.rearrange("p a (h e) -> p a h e", h=n_heads)[:, :, :, s0:s0 + n]
            res.append((ov, iv))
        return res

    wgv = [wg_d[f].rearrange("(dc p) f -> p dc f", p=128) for f in range(2)]
    wuv = [wu_d[f].rearrange("(dc p) f -> p dc f", p=128) for f in range(2)]
    wdv = [wd_d[f].rearrange("(fc p) d -> p fc d", p=128) for f in range(2)]
    winv = win_d.rearrange("(dc p) f -> p dc f", p=128)
    woutv = wout_d.rearrange("(c p) d -> p c d", p=128)
    gt_sem = mk_dsem("gt")
    gt_r = Res()
    with nc.allow_non_contiguous_dma(reason="one-off tiny table build"):
        k.dma(SP, gt_sem, lambda h: h.dma_start(out=gt_s[:, 0:256], in_=relb_d[:, 1:257]), [], [gt_r])
        k.dma(SP, gt_sem, lambda h: h.dma_start(out=bass.AP(gt_s, 256, [[384, 8], [1, 128], [1, 1]]),
                                                in_=bass.AP(relb_d, 256, [[257, 8], [0, 128], [1, 1]])), [], [gt_r])
    barrier()
    checkpoint("prep")

    def sb(name, shape, dt):
        return es.enter_context(nc.sbuf_tensor(name, shape, dt))

    h_t = [sb(f"h{i}", [128, 4, D], F32) for i in range(2)]
    h_r = [[Res(f"h{i}_{tb}") for tb in range(4)] for i in range(2)]
    xn_tok = [sb(f"xntok{i}", [128, D], BF16) for i in range(2)]
    xn_tok_r = [Res() for _ in range(2)]
    xnT = sb("xnT", [128, NDC, T], BF16)
    xnT_r = [Res(f"xnT{tb}") for tb in range(4)]
    U = sb("U", [128, NFC * T], BF16)
    act_v = U[:, :].rearrange("p (a b) -> p a b", a=NFC)
    U_r = [Res(f"U{i}") for i in range(NFC)]
    QaT = U[:, 0:2048].rearrange("p (a b) -> p a b", a=4)
    QbT = U[:, 2048:4096].rearrange("p (a b) -> p a b", a=4)
    OT = U[:, 4096:8192].rearrange("p (a b) -> p a b", a=8)
    PT = [U[:, 8192 + 512 * i: 8192 + 512 * (i + 1)] for i in range(6)]
    QaT_r, QbT_r, OT_r, PT_r = U_r[0:4], U_r[4:8], U_r[8:16], U_r[16:22]
    ring = [sb(f"ring{i}", [128, 4096], BF16) for i in range(3)]
    ring_r = [Res(f"ring{i}") for i in range(3)]
    ring_sem = [mk_dsem(f"ring{i}") for i in range(3)]
    Wd = sb("Wd", [128, NFC * D], BF16)
    Wd_r = Res("Wd")
    wd_sem = mk_dsem("wdl")
    wd_sem_sw = mk_dsem("wdlsw")
    ring_sem_sw = [mk_dsem(f"ringsw{i}") for i in range(3)]
    kv1 = sb("kv1", [128, 4096], F32)
    kv1_r = Res("kv1")
    kv1b = kv1.bitcast(BF16)
    KaT = [sb("KaT0", [128, 4, T], BF16), kv1b[:, 4096:6144].rearrange("p (a b) -> p a b", a=4)]
    KaT_r = [[Res() for _ in range(4)] for _ in range(2)]
    KbT = [sb("KbT0", [128, 2, T], BF16), kv1b[:, 6144:7168].rearrange("p (a b) -> p a b", a=2)]
    KbT_r = [[Res() for _ in range(2)] for _ in range(2)]
    Va = [sb("Va0", [128, 4, 1024], BF16), kv1b[:, 0:4096].rearrange("p (a b) -> p a b", a=4)]
    Va_r = [[Res() for _ in range(4)] for _ in range(2)]
    Vb = [sb(f"Vb{i}", [128, 4, 384], BF16) for i in range(2)]
    Vb_r = [[Res() for _ in range(4)] for _ in range(2)]
    tmpA = [sb(f"tmpA{i}", [128, 1024], F32) for i in range(2)]
    tmpA_r = [Res(f"tmpA{i}") for i in range(2)]
    Ct = sb("Ct", [128, T], F32)
    St = sb("St", [128, T], F32)
    CS_r = Res("CS")
    rsc = sb("rsc", [128, 1024], F32)
    rsc_r = Res("rsc")
    posi = rsc[:, 512:1024].bitcast(I32)
    posi_r = rsc_r
    junk = rsc[:, 0:256].bitcast(BF16)
    junk2 = rsc[:, 0:512].bitcast(BF16)
    junk_r = rsc_r
    pos_sem = mk_dsem("pos")
    gpost = [sb(f"gpost{i}", [128, D], F32) for i in range(4)]
    gpre = sb("gpre", [128, 24], F32)
    BT = sb("BT", [128, 8, 256], BF16)
    ident = sb("ident", [128, 128], BF16)
    antiI = sb("antiI", [128, 128], BF16)
    cst = sb("cst", [128, 8], F32)
    esink = sb("esink", [128, 8], F32)
    epsb = sb("epsb", [128, 1], F32)
    NACC = 0
    NROT = 48
    stats = sb("stats", [128, NACC + NROT], F32)
    rot_r = [Res() for _ in range(NROT)]
    const_r = Res("const")
    x_sem = [mk_dsem(f"xld{i}") for i in range(2)]
    out_sem = [mk_dsem(f"ost{i}") for i in range(2)]
    su_sem = mk_dsem("setup")

    su_res = []

    def setup_dma(out_ap, in_ap, res):
        with nc.allow_non_contiguous_dma(reason="one-off small constant loads"):
            k.dma(SP, su_sem, lambda h: h.dma_start(out=out_ap, in_=in_ap), [], [res])
        su_res.append(res)

    cst_r = Res()
    setup_dma(cst[:, :], consts_d[:, 256:264], cst_r)
    idst_r = h_r[0][0]
    setup_dma(h_t[0][:, 0, 0:256], consts_d[:, 0:256], idst_r)
    gp_r = [Res() for _ in range(4)]
    for i, gi in enumerate((1, 3, 5, 6)):
        setup_dma(gpost[i][:, :], bass.AP(gains_d, gi * D, [[0, 128], [1, D]]), gp_r[i])
    gpre_r = Res()
    with nc.allow_non_contiguous_dma(reason="tiny one-off gain vector transposed load"):
        for i, gi in enumerate((0, 2, 4)):
            k.dma(SP, su_sem, lambda h, i=i, gi=gi: h.dma_start(
                out=bass.AP(gpre, i * 8, [[24, 128], [1, 8], [1, 1]]), in_=bass.AP(gains_d, gi * D, [[1, 128], [128, 8], [1, 1]])), [], [gpre_r])
    su_res.append(gpre_r)
    sk_r = Res()
    setup_dma(esink[:, :], bass.AP(sinks_d, 0, [[0, 128], [1, 8]]), sk_r)
    hank = tmpA
    cc = sb("cc", [128, 8], F32)
    cc_r = Res()
    setup_dma(cc[:, :], bass.AP(relb_d, 256, [[0, 128], [257, 8]]), cc_r)
    hk_r = [Res(), Res()]
    for half in range(2):
        setup_dma(tmpA[half][:, :].rearrange("p (h q) -> p h q", h=4),
                  bass.AP(gt_s, half * 4 * 384, [[1, 128], [384, 4], [1, 256]]), hk_r[half])
    for r_ in su_res:
        r_.w = Ev(su_sem, su_sem.cnt)
    k.op(DVE, lambda h: h.tensor_copy(out=ident[:, :], in_=h_t[0][:, 0, 0:128]), [idst_r], [const_r])
    k.op(DVE, lambda h: h.tensor_copy(out=antiI[:, :], in_=h_t[0][:, 0, 128:256]), [idst_r], [const_r])
    k.op(POOL, lambda h: h.memset(epsb[:, :], EPS), [], [const_r])
    k.op(POOL, lambda h: h.memset(stats[:, :], 0.0), [], [const_r])
    for i in range(2):
        k.op(POOL, lambda h, i=i: h.memset(Va[i][:, :, :], 1.0), [], [r for r in Va_r[i]])
        k.op(POOL, lambda h, i=i: h.memset(Vb[i][:, :, :], 1.0), [], [r for r in Vb_r[i]])
    k.op(ACT, lambda h: h.activation(out=esink[:, :], in_=esink[:, :], func=AF.Exp), [sk_r], [sk_r])
    k.op(DVE, lambda h: h.tensor_scalar(out=gpost[0][:, :], in0=gpost[0][:, :], scalar1=0.5, scalar2=None, op0=ALU.mult), [gp_r[0]], [gp_r[0]])
    k.op(DVE, lambda h: h.tensor_scalar(out=gpost[2][:, :], in0=gpost[2][:, :], scalar1=0.5, scalar2=None, op0=ALU.mult), [gp_r[2]], [gp_r[2]])
    hb16 = PT
    for hd in range(8):
        half, j = hd // 4, hd % 4
        src = tmpA[half][:, j * 256:(j + 1) * 256]
        dst = PT[hd // 2][:, (hd % 2) * 256:(hd % 2 + 1) * 256]
        k.op(DVE, lambda h, src=src, dst=dst, hd=hd: h.tensor_scalar(
            out=dst, in0=src, scalar1=cc[:, hd:hd + 1], scalar2=8.0, op0=ALU.subtract, op1=ALU.mult),
            [hk_r[half], cc_r], [PT_r[hd // 2]])
    for hd in range(8):
        b = next_bank("mm")
        dst = PT[hd // 2][:, (hd % 2) * 256:(hd % 2 + 1) * 256]
        k.op(PE, lambda h, b=b, dst=dst: h.matmul(ps[b][:, 0:256], antiI[:, :], dst, start=True, stop=True),
             [const_r, PT_r[hd // 2]], [ps_res[b]])
        k.op(DVE, lambda h, b=b, hd=hd: h.tensor_copy(out=BT[:, hd, :], in_=ps[b][:, 0:256]), [ps_res[b]], [const_r])
    k.op(POOL, lambda h: h.memset(BT[64:128, :, 0:64], NEG), [const_r], [const_r])
    k.op(ACT, lambda h: h.activation(out=BT[:, :, :], in_=BT[:, :, :], func=AF.Exp, scale=0.125), [const_r], [const_r])
    barrier()
    checkpoint("setup")

    pieces = []
    piece_loaded = [0]
    piece_used = [0]
    scr = {}

    def scr_res(key):
        if key not in scr:
            scr[key] = Res(str(key))
        return scr[key]
    wd_scr_r = [[Res() for _ in range(11)] for _ in range(2)]

    def piece_src(kind, f=None, p=None):
        r_ = scr_res((kind, f, p))
        if kind == "gu":
            return gu_s[f][p], (lambda tns: tns[:, :]), r_
        cols = {"qa": (C_QA, 512), "ka": (C_KA, 512), "qb": (C_QB, 512), "qbs": (C_QBS, 512), "kb4": (C_KB4, 512),
                "va": (C_VA, 512), "vb": (C_VB, 128)}
        if kind in cols:
            c0, w = cols[kind]
            return win_s[:, :, c0:c0 + w], v3(8, w), r_
        if kind == "wo":
            return wout_s[:, :, p * 512:(p + 1) * 512], v3(8, 512), r_
        raise ValueError(kind)

    def tile_piece_list():
        lst = [("gu", 0, p) for p in range(11)]
        lst += [("va", None, None), ("vb", None, None), ("qa", None, None), ("ka", None, None), ("qb", None, None),
                ("qbs", None, None), ("kb4", None, None)]
        lst += [("wo", None, 0), ("wo", None, 1)]
        lst += [("gu", 1, p) for p in range(11)]
        return lst
    NP0 = len(tile_piece_list())

    for t in range(n_tiles):
        for it in tile_piece_list():
            pieces.append(piece_src(*it))

    order0 = [("qbs", None, None), ("kb4", None, None)]
    jit_pos = {it: j for j, it in enumerate(order0)}

    def jit_spec(kind, f, p):
        ident_v = (lambda d: d)
        if kind == "gu":
            lo, hi = p * 256, (p + 1) * 256
            return ([(lambda tns: tns[:, 0:2048].rearrange("p (a b) -> p a b", a=8), wgv[f][:, :, lo:hi]),
                     (lambda tns: tns[:, 2048:4096].rearrange("p (a b) -> p a b", a=8), wuv[f][:, :, lo:hi])],
                    [(flat(4096), flat(4096))], [(gu_s[f][p], flat(4096), scr_res((kind, f, p)))])
        if kind == "wd":
            f0, f1 = 4 * p, min(4 * p + 4, NFC)
            n = (f1 - f0) * D
            return ([(v3(f1 - f0, D), wdv[f][:, f0:f1, :])], [(ident_v, flat(n))],
                    [(wd_s[f][:, f0 * D:f1 * D], ident_v, wd_scr_r[f][p])])
        if kind in ("qa", "ka", "qb", "va"):
            c_src = {"qa": 0, "ka": 512, "qb": 1536, "va": 1024}[kind]
            c_dst = {"qa": C_QA, "ka": C_KA, "qb": C_QB, "va": C_VA}[kind]
            return ([(v3(8, 512), winv[:, :, c_src:c_src + 512])], [(flat(4096), flat(4096))],
                    [(win_s[:, :, c_dst:c_dst + 512], v3(8, 512), scr_res((kind, f, p)))])
        if kind == "qbs":
            return ([(v3(8, 512), winv[:, :, 1536:2048])], head_swap_casts(8, 0, 0, 512, 512),
                    [(win_s[:, :, C_QBS:C_QBS + 512], v3(8, 512), scr_res((kind, f, p)))])
        if kind == "kb4":
            cs_ = [(v3s(8, 512, 0, 128), v3(8, 128))]
            cs_ += head_swap_casts(2, 0, 128, 128, 512)
            cs_ += [(v3s(8, 512, 256, 320), v3s(8, 128, 64, 128)), (v3s(8, 512, 320, 384), v3s(8, 128, 0, 64))]
            cs_ += head_swap_casts(1, 64, 384, 128, 512) + head_swap_casts(1, 0, 448, 128, 512)
            return ([(v3(8, 128), winv[:, :, 2048:2176])], cs_,
                    [(win_s[:, :, C_KB4:C_KB4 + 512], v3(8, 512), scr_res((kind, f, p)))])
        if kind == "vb":
            return ([(v3(8, 128), winv[:, :, 2176:2304])], [(flat(1024), flat(1024))],
                    [(win_s[:, :, C_VB:C_VB + 128], v3(8, 128), scr_res((kind, f, p)))])
        if kind == "wo":
            return ([(v3(8, 512), woutv[:, :, p * 512:(p + 1) * 512])], [(flat(4096), flat(4096))],
                    [(wout_s[:, :, p * 512:(p + 1) * 512], v3(8, 512), scr_res((kind, f, p)))])
        raise ValueError(kind)

    stage_ap = [h_t[1][:, :, :].rearrange("p a b -> p (a b)"), kv1[:, :]]
    stage_res = [h_r[1], [kv1_r]]
    jit_sem = [mk_dsem("jl0"), mk_dsem("jl1")]
    jit_issued = [0]
    slot_store_sem = [mk_dsem(f"sst{i}") for i in range(3)]
    wd_store_sem = [[mk_dsem(f"wst{f}_{c}") for c in range(11)] for f in range(2)]

    def jit_issue_load(j):
        if j >= len(order0) or j < jit_issued[0]:
            return
        assert j == jit_issued[0]
        s_ = j % 2
        loads, _, _ = jit_spec(*order0[j])
        for (sv, src) in loads:
            k.dma(SP, jit_sem[s_], lambda h, sv=sv, src=src: h.dma_start(out=sv(stage_ap[s_]), in_=src), [], stage_res[s_])
        jit_issued[0] += 1

    def jit_materialize(item, dest, dest_res, store_sem):
        j = jit_pos[item]
        jit_issue_load(j)
        jit_issue_load(j + 1)
        s_ = j % 2
        _, casts, stores = jit_spec(*item)
        for (ov, iv) in casts:
            k.op(DVE, lambda h, ov=ov, iv=iv: h.tensor_copy(out=ov(dest), in_=iv(stage_ap[s_])), stage_res[s_], [dest_res])
        jit_issue_load(j + 2)
        for (dst, ov, r_) in stores:
            k.dma(SP, store_sem, lambda h, dst=dst, ov=ov: h.dma_start(out=dst, in_=ov(dest)), [dest_res], [r_])

    def ensure_loaded(upto):
        while piece_loaded[0] <= min(upto, len(pieces) - 1):
            i = piece_loaded[0]
            piece_loaded[0] += 1
            if i < NP0:
                kind, f_, p_ = tile0_items[i]
                if kind in ("qbs", "kb4"):
                    continue
                s = i % 3
                for (view, src32) in cast_srcs(kind, f_, p_):
                    k.dma(POOL, ring_sem_sw[s], lambda h, s=s, view=view, src32=src32: h.dma_start(out=view(ring[s]), in_=src32),
                          [], [ring_r[s]])
                src, view, r_ = pieces[i]
                k.dma(SP, slot_store_sem[s], lambda h, s=s, src=src, view=view: h.dma_start(out=src, in_=view(ring[s])),
                      [ring_r[s]], [r_])
                continue
            s = i % 3
            src, view, r_ = pieces[i]
            k.dma(SP, ring_sem[s], lambda h, s=s, src=src, view=view: h.dma_start(out=view(ring[s]), in_=src), [r_], [ring_r[s]])

    tile0_items = tile_piece_list()

    def cast_srcs(kind, f, p):
        if kind == "gu":
            lo, hi = p * 256, (p + 1) * 256
            return [(lambda tns: tns[:, 0:2048].rearrange("p (a b) -> p a b", a=8), wgv[f][:, :, lo:hi]),
                    (lambda tns: tns[:, 2048:4096].rearrange("p (a b) -> p a b", a=8), wuv[f][:, :, lo:hi])]
        if kind in ("qa", "ka", "qb", "va"):
            c_src = {"qa": 0, "ka": 512, "qb": 1536, "va": 1024}[kind]
            return [(v3(8, 512), winv[:, :, c_src:c_src + 512])]
        if kind == "vb":
            return [(v3(8, 128), winv[:, :, 2176:2304])]
        if kind == "wo":
            return [(v3(8, 512), woutv[:, :, p * 512:(p + 1) * 512])]
        raise ValueError(kind)

    def take_piece(la=2):
        i = piece_used[0]
        piece_used[0] += 1
        if i < NP0 and tile0_items[i][0] in ("qbs", "kb4"):
            jit_materialize(tile0_items[i], ring[i % 3], ring_r[i % 3], slot_store_sem[i % 3])
        ensure_loaded(i + la)
        return i % 3

    stat_i = [0]
    rot_i = [0]

    def acc_col():
        return stat_col()

    def stat_col():
        j = rot_i[0] % NROT
        rot_i[0] += 1
        return stats[:, NACC + j:NACC + j + 1], rot_r[j]

    def rstd_from(ss_ap, ss_r):
        rs_ap, rs_r = stat_col()
        k.op(ACT, lambda h: h.activation(out=rs_ap, in_=ss_ap, func=AF.Sqrt, bias=epsb[:, :], scale=1.0), [ss_r, const_r], [rs_r])
        rd_ap, rd_r = stat_col()
        k.op(DVE, lambda h: h.reciprocal(out=rd_ap, in_=rs_ap), [rs_r], [rd_r])
        return rd_ap, rd_r

    xn_rot = [0]

    def prenorm_stats(hb, tb, scale_eng=None):
        i = xn_rot[0] % 2
        xn_rot[0] += 1
        ss_ap, ss_r = acc_col()
        src = h_t[hb][:, tb, :]
        k.op(ACT, lambda h: h.activation(out=xn_tok[i][:, :], in_=src, func=AF.Square, scale=1.0 / 32.0, accum_out=ss_ap),
             [h_r[hb][tb], const_r], [xn_tok_r[i], ss_r])
        rd_ap, rd_r = rstd_from(ss_ap, ss_r)
        if scale_eng is POOL or scale_eng is DVE:
            k.op(scale_eng, lambda h: h.tensor_scalar(out=xn_tok[i][:, :], in0=src, scalar1=rd_ap, scalar2=None, op0=ALU.mult),
                 [h_r[hb][tb], rd_r], [xn_tok_r[i]])
        else:
            k.op(ACT, lambda h: h.activation(out=xn_tok[i][:, :], in_=src, func=AF.Copy, scale=rd_ap),
                 [h_r[hb][tb], rd_r], [xn_tok_r[i]])
        return i

    def prenorm_T(i, tb, gi):
        b = next_bank("tr")
        pst = ps[b].bitcast(BF16)

        def f(h):
            ins = None
            for dc in range(NDC):
                ins = h.transpose(pst[:, dc * 128:(dc + 1) * 128], xn_tok[i][:, dc * 128:(dc + 1) * 128], ident[:, :])
            return ins
        k.op(PE, f, [xn_tok_r[i], const_r], [ps_res[b]])
        in0 = pst[:, :].rearrange("p (a b) -> p a b", a=NDC)
        in1 = bass.AP(gpre, gi * 8, [[24, 128], [1, 8], [0, 128]])
        k.op(DVE, lambda h: h.tensor_tensor(out=xnT[:, :, tb * 128:(tb + 1) * 128], in0=in0, in1=in1, op=ALU.mult),
             [ps_res[b], gpre_r], [xnT_r[tb]])

    def mm_group(b, out_ap, pairs, reads, start=True):
        def f(h):
            ins = None
            n = len(pairs)
            for j, (l, r) in enumerate(pairs):
                ins = h.matmul(out_ap, l, r, start=(start and j == 0), stop=(j == n - 1))
            return ins
        return k.op(PE, f, reads, [ps_res[b]])

    def postnorm(hb, tb, banks, gp):
        i = tb % 2
        b0, b1 = banks
        assert b1 == b0 + 1
        st_ap, st_r = acc_col()
        fv = ps_all[:, b0 * 512:(b0 + 2) * 512]
        k.op(ACT, lambda h: h.activation(out=junk2, in_=fv, func=AF.Square, scale=1.0 / 32.0, accum_out=st_ap),
             [ps_res[b0], ps_res[b1], const_r], [junk_r, st_r])
        k.op(DVE, lambda h: h.tensor_tensor(out=tmpA[i][:, :], in0=fv, in1=gpost[gp][:, :], op=ALU.mult),
             [ps_res[b0], ps_res[b1], gp_r[gp]], [tmpA_r[i]])
        rd_ap, rd_r = rstd_from(st_ap, st_r)
        dst = h_t[hb][:, tb, :]
        k.op(DVE, lambda h: h.scalar_tensor_tensor(out=dst, in0=tmpA[i][:, :], scalar=rd_ap, in1=dst, op0=ALU.mult, op1=ALU.add),
             [tmpA_r[i], rd_r, h_r[hb][tb]], [h_r[hb][tb]])

    gu_state = {}

    def gu_first_partial(wd_f, jit):
        s = take_piece()
        load_wd(wd_f, 0, first_tile=jit)
        gv = ring[s][:, :].rearrange("p (g a b) -> p g a b", g=2, a=8)
        if rot["mm"] % 2:
            rot["mm"] += 1
        banks = [next_bank("mm") for _ in range(4)]
        n = 0
        for j in range(2):
            for g_ in range(2):
                b = banks[n]
                n += 1
                mm_group(b, ps[b][:, 0:384], [(gv[:, g_, dc, j * 128:(j + 1) * 128], xnT[:, dc, 0:384]) for dc in range(NDC)],
                         [ring_r[s]] + xnT_r[0:3])
        gu_state["s"], gu_state["banks"] = s, banks

    def gu_first_rest():
        s, banks = gu_state["s"], gu_state["banks"]
        gv = ring[s][:, :].rearrange("p (g a b) -> p g a b", g=2, a=8)
        n = 0
        for j in range(2):
            bg, bu = banks[n], banks[n + 1]
            for g_, b in ((0, bg), (1, bu)):
                mm_group(b, ps[b][:, 384:512], [(gv[:, g_, dc, j * 128:(j + 1) * 128], xnT[:, dc, 384:512]) for dc in range(NDC)],
                         [ring_r[s], xnT_r[3]])
            n += 2
            fc = j
            i = fc % 2
            k.op(ACT, lambda h, bg=bg, i=i: h.activation(out=tmpA[i][:, 0:512], in_=ps[bg][:, :], func=AF.Silu),
                 [ps_res[bg]], [tmpA_r[i]])
            k.op(DVE, lambda h, bu=bu, i=i, fc=fc: h.tensor_tensor(out=act_v[:, fc, :], in0=ps[bu][:, :], in1=tmpA[i][:, 0:512],
                                                                   op=ALU.mult), [ps_res[bu], tmpA_r[i]], [U_r[fc]])

    def ffn_gate_up(wd_f, mid_hook=None, jit=False, hook_every=False, skip_first=False):
        for p in range(11):
            if skip_first and p == 0:
                continue
            s = take_piece()
            if jit and OPT_FINE_WD:
                load_wd(wd_f, p, first_tile=True)
            elif p in (0, 2, 4, 6):
                load_wd(wd_f, p // 2, first_tile=jit)
            if mid_hook is not None and hook_every:
                mid_hook(p)
            elif mid_hook is not None and p == 6:
                mid_hook()
            gv = ring[s][:, :].rearrange("p (g a b) -> p g a b", g=2, a=8)
            for j in range(2):
                fc = 2 * p + j
                bg, bu = next_bank("mm"), next_bank("mm")
                mm_group(bg, ps[bg][:, :], [(gv[:, 0, dc, j * 128:(j + 1) * 128], xnT[:, dc, :]) for dc in range(NDC)],
                         [ring_r[s]] + xnT_r)
                mm_group(bu, ps[bu][:, :], [(gv[:, 1, dc, j * 128:(j + 1) * 128], xnT[:, dc, :]) for dc in range(NDC)],
                         [ring_r[s]] + xnT_r)
                i = fc % 2
                k.op(ACT, lambda h, bg=bg, i=i: h.activation(out=tmpA[i][:, 0:512], in_=ps[bg][:, :], func=AF.Silu),
                     [ps_res[bg]], [tmpA_r[i]])
                k.op(DVE, lambda h, bu=bu, i=i, fc=fc: h.tensor_tensor(out=act_v[:, fc, :], in0=ps[bu][:, :], in1=tmpA[i][:, 0:512],
                                                                       op=ALU.mult), [ps_res[bu], tmpA_r[i]], [U_r[fc]])

    def load_wd(f, q, first_tile=False):
        if first_tile and OPT_FINE_WD:
            f0, f1 = 2 * q, 2 * q + 2
        else:
            f0, f1 = q * 6, min((q + 1) * 6, NFC)
        c0, c1 = f0 * D, f1 * D
        if first_tile:
            dstv = Wd[:, c0:c1].rearrange("p (a b) -> p a b", a=f1 - f0)
            k.dma(POOL, wd_sem_sw, lambda h: h.dma_start(out=dstv, in_=wdv[f][:, f0:f1, :]), [], [Wd_r])
            k.dma(SP, wd_store_sem[f][q], lambda h: h.dma_start(out=wd_s[f][:, c0:c1], in_=Wd[:, c0:c1]), [Wd_r], [wd_scr_r[f][q]])
        else:
            k.dma(SP, wd_sem, lambda h: h.dma_start(out=Wd[:, c0:c1], in_=wd_s[f][:, c0:c1]), wd_scr_r[f], [Wd_r])

    def ffn_down(tb):
        banks = []
        if rot["mm"] % 2:
            rot["mm"] += 1
        for hf in range(2):
            b = next_bank("mm")
            mm_group(b, ps[b][:, :], [(act_v[:, fc, tb * 128:(tb + 1) * 128], Wd[:, fc * D + hf * 512: fc * D + (hf + 1) * 512])
                                      for fc in range(NFC)], [Wd_r] + U_r)
            banks.append(b)
        return banks

    def rope_tables():
        ang, kq = rsc[:, 0:512], rsc[:, 512:1024]
        invf, sgn = cst[:, 0:1], cst[:, 1:2]
        rr, cs = [rsc_r], [CS_r]
        k.op(DVE, lambda h: h.tensor_copy(out=kq, in_=posi), [posi_r], rr)
        k.op(DVE, lambda h: h.tensor_scalar(out=ang, in0=kq, scalar1=invf, scalar2=None, op0=ALU.mult), rr + [cst_r], rr)
        yield
        k.op(DVE, lambda h: h.tensor_scalar(out=St[:, :], in0=ang, scalar1=float(1.0 / TWO_PI), scalar2=None, op0=ALU.mult), rr + cs, cs)
        k.op(DVE, lambda h: h.tensor_copy(out=kq.bitcast(I32), in_=St[:, :]), rr + cs, rr)
        k.op(DVE, lambda h: h.tensor_copy(out=St[:, :], in_=kq.bitcast(I32)), rr + cs, cs)
        yield
        k.op(DVE, lambda h: h.scalar_tensor_tensor(out=Ct[:, :], in0=St[:, :], scalar=-CW1, in1=ang, op0=ALU.mult, op1=ALU.add), rr + cs, cs)
        k.op(DVE, lambda h: h.scalar_tensor_tensor(out=Ct[:, :], in0=St[:, :], scalar=-CW2, in1=Ct[:, :], op0=ALU.mult, op1=ALU.add), cs, cs)
        yield
        k.op(DVE, lambda h: h.tensor_scalar(out=Ct[:, :], in0=Ct[:, :], scalar1=-PI_LO, scalar2=PI_LO, op0=ALU.max, op1=ALU.min), cs, cs)
        k.op(ACT, lambda h: h.activation(out=St[:, :], in_=Ct[:, :], func=AF.Sin, scale=sgn), cs + [cst_r], cs)
        yield
        k.op(DVE, lambda h: h.tensor_scalar(out=ang, in0=ang, scalar1=float(0.5 * np.pi), scalar2=None, op0=ALU.add), rr, rr)
        k.op(DVE, lambda h: h.tensor_scalar(out=Ct[:, :], in0=ang, scalar1=float(1.0 / TWO_PI), scalar2=None, op0=ALU.mult), rr + cs, cs)
        yield
        k.op(DVE, lambda h: h.tensor_copy(out=kq.bitcast(I32), in_=Ct[:, :]), rr + cs, rr)
        k.op(DVE, lambda h: h.tensor_copy(out=kq, in_=kq.bitcast(I32)), rr, rr)
        yield
        k.op(DVE, lambda h: h.scalar_tensor_tensor(out=Ct[:, :], in0=kq, scalar=-CW1, in1=ang, op0=ALU.mult, op1=ALU.add), rr + cs, cs)
        k.op(DVE, lambda h: h.scalar_tensor_tensor(out=Ct[:, :], in0=kq, scalar=-CW2, in1=Ct[:, :], op0=ALU.mult, op1=ALU.add), rr + cs, cs)
        yield
        k.op(DVE, lambda h: h.tensor_scalar(out=Ct[:, :], in0=Ct[:, :], scalar1=-PI_LO, scalar2=PI_LO, op0=ALU.max, op1=ALU.min), cs, cs)
        k.op(ACT, lambda h: h.activation(out=Ct[:, :], in_=Ct[:, :], func=AF.Sin), cs, cs)
        yield

    def proj_block(s, width, blk):
        v = ring[s][:, 0:8 * width].rearrange("p (a b) -> p a b", a=8)
        b = next_bank("mm")
        mm_group(b, ps[b][:, :], [(v[:, dc, blk * 128:(blk + 1) * 128], xnT[:, dc, :]) for dc in range(NDC)], [ring_r[s]] + xnT_r)
        return b

    def rope_apply(bx, by, dst_ap, dst_r, i):
        k.op(DVE, lambda h: h.tensor_tensor(out=tmpA[i][:, 0:512], in0=ps[bx][:, :], in1=Ct[:, :], op=ALU.mult),
             [ps_res[bx], CS_r], [tmpA_r[i]])
        k.op(DVE, lambda h: h.tensor_tensor(out=tmpA[i][:, 512:1024], in0=ps[by][:, :], in1=St[:, :], op=ALU.mult),
             [ps_res[by], CS_r, tmpA_r[i]], [tmpA_r[i]])
        k.op(POOL, lambda h: h.tensor_tensor(out=dst_ap, in0=tmpA[i][:, 0:512], in1=tmpA[i][:, 512:1024], op=ALU.add),
             [tmpA_r[i]], [dst_r])

    def mixer_proj(cur):
        s = take_piece()
        for blk in range(4):
            b = proj_block(s, 512, blk)
            k.op(ACT, lambda h, b=b, blk=blk: h.copy(out=QaT[:, blk, :], in_=ps[b][:, :]), [ps_res[b]], [QaT_r[blk]])
        s = take_piece()
        for blk in range(4):
            b = proj_block(s, 512, blk)
            k.op(DVE, lambda h, b=b, blk=blk: h.tensor_copy(out=KaT[cur][:, blk, :], in_=ps[b][:, :]), [ps_res[b]], [KaT_r[cur][blk]])
        s1 = take_piece()
        s2 = take_piece(1)
        for blk in range(4):
            bx = proj_block(s1, 512, blk)
            by = proj_block(s2, 512, blk)
            rope_apply(bx, by, QbT[:, blk, :], QbT_r[blk], blk % 2)
        s = take_piece()
        for v in range(2):
            bx = proj_block(s, 512, 2 * v)
            by = proj_block(s, 512, 2 * v + 1)
            rope_apply(bx, by, KbT[cur][:, v, :], KbT_r[cur][v], v)

    vslots = {}

    def v_take():
        vslots["a"] = take_piece()
        vslots["b"] = take_piece(1)

    def v_proj(cur, tb):
        sva, svb = vslots["a"], vslots["b"]
        vva = ring[sva][:, 0:4096].rearrange("p (a b) -> p a b", a=8)
        vvb = ring[svb][:, 0:1024].rearrange("p (a b) -> p a b", a=8)
        ba = next_bank("mm")
        mm_group(ba, ps[ba][:, :], [(xnT[:, dc, tb * 128:(tb + 1) * 128], vva[:, dc, :]) for dc in range(NDC)],
                 [ring_r[sva], xnT_r[tb]])
        bb = next_bank("mm")
        mm_group(bb, ps[bb][:, 0:128], [(xnT[:, dc, tb * 128:(tb + 1) * 128], vvb[:, dc, :]) for dc in range(NDC)],
                 [ring_r[svb], xnT_r[tb]])
        src = ps[ba][:, :].rearrange("p (h two e) -> p h two e", h=4, two=2)
        dstv = Va[cur][:, tb, :].rearrange("p (h four e) -> p h four e", h=4, four=4)
        k.op(ACT, lambda h: h.copy(out=dstv[:, :, 0, :], in_=src[:, :, 0, :]), [ps_res[ba]], [Va_r[cur][tb]])
        k.op(DVE, lambda h: h.tensor_copy(out=dstv[:, :, 3, :], in_=src[:, :, 1, :]), [ps_res[ba]], [Va_r[cur][tb]])
        srcb = ps[bb][:, 0:128].rearrange("p (g e) -> p g e", g=2)
        dstb = Vb[cur][:, tb, :].rearrange("p (g three e) -> p g three e", g=2, three=3)[:, :, 1, :]
        k.op(DVE, lambda h: h.tensor_copy(out=dstb, in_=srcb), [ps_res[bb]], [Vb_r[cur][tb]])

    pt_rot = [0]

    def attention(cur, prev, first):
        steps = []
        for hd in range(8):
            hbk, pb = hd // 2, (hd % 2) * 64
            kts = [4, 5, 6, 7] if first else [3, 0, 1, 2, 4, 5, 6, 7]
            for n, kt in enumerate(kts):
                buf = cur if kt >= 4 else prev
                tbk = kt % 4
                qlo, qhi = max(0, 2 * kt - 8), min(7, 2 * kt + 1)
                extra = []
                if kt >= 3:
                    ilo, ihi = max(0, 2 * kt - 8), min(7, 2 * kt - 5)
                    extra.append((BT[:, hd, (ilo - (2 * kt - 8)) * 64:(ihi - (2 * kt - 8) + 1) * 64], (ilo - qlo) * 64, (ihi - qlo + 1) * 64))
                zero = []
                if 2 * kt + 1 <= 7:
                    i_ = 2 * kt + 1
                    zero.append((slice(0, 64), (i_ - qlo) * 64))
                steps.append(dict(zero=zero,
                    K=KaT[buf][pb:pb + 64, hbk, tbk * 128:(tbk + 1) * 128], K_r=KaT_r[buf][hbk],
                    Q=QaT[pb:pb + 64, hbk, qlo * 64:(qhi + 1) * 64], Q_r=QaT_r[hbk],
                    V=Va[buf][:, tbk, hd * 128:(hd + 1) * 128], V_r=Va_r[buf][tbk],
                    q0=qlo * 64, ncols=(qhi + 1 - qlo) * 64, extra=extra, head=("a", hd), first=(n == 0), last=(n == len(kts) - 1),
                    odd=hd % 2, ot=OT[pb:pb + 64, hbk, :], ot_r=OT_r[hbk], sink=None))
        for hd in range(8):
            hbk, pb, g = hd // 2, (hd % 2) * 64, hd // 4
            var = 0 if pb == g * 64 else 1
            kts = [4, 5, 6, 7] if first else [3, 4, 5, 6, 7]
            for n, kt in enumerate(kts):
                buf = cur if kt >= 4 else prev
                tbk = kt % 4
                ilo, ihi = max(0, 2 * kt - 8), min(7, 2 * kt - 5)
                extra = []
                zero = []
                if 2 * kt - 8 >= 0:
                    zero.append((slice(64, 128), 0))
                if 2 * kt - 5 <= 7:
                    zero.append((slice(0, 64), (2 * kt - 5 - ilo) * 64))
                voff = g * 192 + (64 if hd % 2 == 0 else 0)
                steps.append(dict(zero=zero,
                    K=KbT[buf][pb:pb + 64, var, tbk * 128:(tbk + 1) * 128], K_r=KbT_r[buf][var],
                    Q=QbT[pb:pb + 64, hbk, ilo * 64:(ihi + 1) * 64], Q_r=QbT_r[hbk],
                    V=Vb[buf][:, tbk, voff:voff + 128], V_r=Vb_r[buf][tbk],
                    q0=ilo * 64, ncols=(ihi + 1 - ilo) * 64, extra=extra, head=("b", hd), first=(n == 0), last=(n == len(kts) - 1),
                    odd=hd % 2, ot=OT[pb:pb + 64, 4 + hbk, :], ot_r=OT_r[4 + hbk], sink=hd))

        def interleave(lst):
            out = []
            by_head = {}
            order = []
            for sp in lst:
                if sp["head"] not in by_head:
                    by_head[sp["head"]] = []
                    order.append(sp["head"])
                by_head[sp["head"]].append(sp)
            for j in range(0, len(order), 2):
                a, b = by_head[order[j]], by_head[order[j + 1]]
                for x, y in zip(a, b):
                    out += [x, y]
            return out
        steps = interleave(steps)
        cur_o = {}

        pair_rot = [0]

        def emit_st_pair(spa, spb):
            bA = ST_BANKS[(pair_rot[0] % 2) * 2]
            piA = (pair_rot[0] % 3) * 2
            pair_rot[0] += 1
            n = spa["ncols"]
            assert spb["ncols"] == n
            for sp, b, pi in ((spa, bA, piA), (spb, bA + 1, piA + 1)):
                sp["stb"], sp["pt"] = b, pi
                k.op(PE, lambda h, sp=sp, b=b: h.matmul(ps[b][:, 0:n], sp["K"], sp["Q"], start=True, stop=True),
                     [sp["K_r"], sp["Q_r"]], [ps_res[b]])
            outv = U[:, 8192 + 512 * piA: 8192 + 512 * (piA + 2)].rearrange("p (two n) -> p two n", two=2)[:, :, 0:n]
            k.op(ACT, lambda h: h.activation(out=outv, in_=pair_ap(bA, n), func=AF.Exp, scale=0.125),
                 [ps_res[bA], ps_res[bA + 1]], [PT_r[piA], PT_r[piA + 1]])
            for sp in (spa, spb):
                pi = sp["pt"]
                for (tab, c0, c1) in sp["extra"]:
                    k.op(POOL, lambda h, tab=tab, c0=c0, c1=c1, pi=pi: h.tensor_tensor(out=PT[pi][:, c0:c1], in0=PT[pi][:, c0:c1], in1=tab, op=ALU.mult),
                         [PT_r[pi], const_r], [PT_r[pi]])
                for (rows, c0) in sp["zero"]:
                    k.op(POOL, lambda h, rows=rows, c0=c0, pi=pi: h.memset(PT[pi][rows, c0:c0 + 64], 0.0), [PT_r[pi]], [PT_r[pi]])

        def emit_pv(sp):
            if sp["first"]:
                cur_o[sp["head"]] = next_bank("o")
            ob = cur_o[sp["head"]]
            pi = sp["pt"]
            k.op(PE, lambda h: h.matmul(ps[ob][:, sp["q0"]:sp["q0"] + sp["ncols"]], sp["V"], PT[pi][:, 0:sp["ncols"]],
                                        start=sp["first"], stop=sp["last"], skip_group_check=True),
                 [sp["V_r"], PT_r[pi]], [ps_res[ob]])
            if sp["last"]:
                den = slice(0, 64) if sp["odd"] else slice(64, 128)
                dat = slice(64, 128) if sp["odd"] else slice(0, 64)
                i = sp["head"][1] % 2
                rec = tmpA[i][den, 0:512]
                use_act = (sp["sink"] is not None) or (OPT_BAL_RECIP and sp["head"][1] % 2 == 1)
                if sp["sink"] is not None:
                    hd = sp["sink"]
                    if use_act:
                        k.op(ACT, lambda h: h.activation(out=rec, in_=ps[ob][den, :], func=AF.Ln, bias=esink[den, hd:hd + 1], scale=1.0),
                             [ps_res[ob], sk_r], [tmpA_r[i]])
                        k.op(ACT, lambda h: h.activation(out=rec, in_=rec, func=AF.Exp, scale=-1.0), [tmpA_r[i]], [tmpA_r[i]])
                    else:
                        k.op(DVE, lambda h: h.tensor_scalar(out=rec, in0=ps[ob][den, :], scalar1=esink[den, hd:hd + 1], scalar2=None,
                                                            op0=ALU.add), [ps_res[ob], sk_r], [tmpA_r[i]])
                        k.op(DVE, lambda h: h.reciprocal(out=rec, in_=rec), [tmpA_r[i]], [tmpA_r[i]])
                elif use_act:
                    k.op(ACT, lambda h: h.activation(out=rec, in_=ps[ob][den, :], func=AF.Ln), [ps_res[ob]], [tmpA_r[i]])
                    k.op(ACT, lambda h: h.activation(out=rec, in_=rec, func=AF.Exp, scale=-1.0), [tmpA_r[i]], [tmpA_r[i]])
                else:
                    k.op(DVE, lambda h: h.reciprocal(out=rec, in_=ps[ob][den, :]), [ps_res[ob]], [tmpA_r[i]])
                k.op(DVE, lambda h: h.tensor_tensor(out=sp["ot"], in0=ps[ob][dat, :], in1=rec, op=ALU.mult),
                     [ps_res[ob], tmpA_r[i]], [sp["ot_r"]])

        LAG = 4
        for n in range(0, len(steps), 2):
            emit_st_pair(steps[n], steps[n + 1])
            if n >= LAG:
                emit_pv(steps[n - LAG])
                emit_pv(steps[n - LAG + 1])
        for sp in steps[len(steps) - LAG:]:
            emit_pv(sp)

    def reset_mm8():
        rot["mm8"] = 0

    def wout_tb(tb, s0, s1):
        banks = []
        for hf, s in enumerate((s0, s1)):
            v = ring[s][:, 0:4096].rearrange("p (a b) -> p a b", a=8)
            b = next_bank("mm8")
            mm_group(b, ps[b][:, :], [(OT[:, c, tb * 128:(tb + 1) * 128], v[:, c, :]) for c in range(8)], [ring_r[s]] + OT_r)
            banks.append(b)
        return banks

    def final_norm_store(hb, tb, seq, s0):
        ss_ap, ss_r = acc_col()
        src = h_t[hb][:, tb, :]
        k.op(ACT, lambda h: h.activation(out=tmpA[tb % 2][:, :], in_=src, func=AF.Square, scale=1.0 / 32.0, accum_out=ss_ap),
             [h_r[hb][tb], const_r], [tmpA_r[tb % 2], ss_r])
        rd_ap, rd_r = rstd_from(ss_ap, ss_r)
        k.op(DVE, lambda h: h.scalar_tensor_tensor(out=src, in0=src, scalar=rd_ap, in1=gpost[3][:, :], op0=ALU.mult, op1=ALU.mult),
             [h_r[hb][tb], rd_r, gp_r[3]], [h_r[hb][tb]])
        dst = out_d[seq, s0 + tb * 128: s0 + (tb + 1) * 128, :]
        k.dma(SP, out_sem[hb], lambda h: h.dma_start(out=dst, in_=src), [h_r[hb][tb]], [])

    def load_x(t):
        hb = t % 2
        seq, s0 = t // TILES_PER_SEQ, (t % TILES_PER_SEQ) * T
        srcv = x_d[seq, s0:s0 + T, :].rearrange("(tb p) d -> p tb d", p=128)
        k.dma(SP, x_sem[hb], lambda h: h.dma_start(out=h_t[hb][:, :, :], in_=srcv), [], h_r[hb])

    def load_pos(t):
        seq, s0 = t // TILES_PER_SEQ, (t % TILES_PER_SEQ) * T
        k.dma(SP, pos_sem, lambda h: h.dma_start(out=posi, in_=bass.AP(pos_d, seq * SEQ + s0, [[0, 128], [1, T]])), [], [posi_r])

    def after_tile0_jit():
        for r_ in Va_r[1] + KaT_r[1] + KbT_r[1]:
            r_.w = kv1_r.w
            r_.r = dict(kv1_r.r)
        k.op(POOL, lambda h: h.memset(Va[1][:, :, :], 1.0), [], Va_r[1])

    load_x(0)
    load_pos(0)
    if n_tiles > 1:
        pass
    try:
        _tile_loop(locals())
    except StopBuild:
        dstv = out_d[0, 0:T, :].rearrange("(tb p) d -> p tb d", p=128)
        k.dma(SP, out_sem[0], lambda h: h.dma_start(out=dstv, in_=h_t[0][:, :, :]), h_r[0], [])
        flat1 = h_t[1][:, :, :].rearrange("p a b -> p (a b)")
        k.op(DVE, lambda h: h.tensor_copy(out=flat1, in_=U[:, 4096:8192]), OT_r, h_r[1])
        k.dma(SP, out_sem[0], lambda h: h.dma_start(out=out_d[1, 0:T, :].rearrange("(p a) d -> p (a d)", a=4), in_=flat1), h_r[1], [])
        k.op(DVE, lambda h: h.tensor_copy(out=flat1, in_=U[:, 0:4096]), QaT_r + QbT_r, h_r[1])
        k.dma(SP, out_sem[0], lambda h: h.dma_start(out=out_d[2, 0:T, :].rearrange("(p a) d -> p (a d)", a=4), in_=flat1), h_r[1], [])
        k.op(DVE, lambda h: h.tensor_copy(out=flat1[:, 0:2048], in_=KaT[0][:, :, :].rearrange("p a b -> p (a b)")), KaT_r[0], h_r[1])
        k.op(DVE, lambda h: h.tensor_copy(out=flat1[:, 2048:3072], in_=KbT[0][:, :, :].rearrange("p a b -> p (a b)")), KbT_r[0], h_r[1])
        k.op(DVE, lambda h: h.tensor_copy(out=flat1[:, 3072:4096], in_=Va[0][:, 0, :]), Va_r[0], h_r[1])
        k.dma(SP, out_sem[0], lambda h: h.dma_start(out=out_d[3, 0:T, :].rearrange("(p a) d -> p (a d)", a=4), in_=flat1), h_r[1], [])
    k.wait_all(SP, [Ev(d, d.cnt) for d in out_sem if d.cnt > 0])


def _tile_loop(L):
    globals_needed = ("n_tiles checkpoint load_x load_pos prenorm_stats prenorm_T ensure_loaded piece_used load_wd ffn_gate_up "
                      "ffn_down postnorm rope_tables mixer_proj attention take_piece wout_tb final_norm_store v_take v_proj after_tile0_jit reset_mm8 gu_first_partial gu_first_rest").split()
    (n_tiles, checkpoint, load_x, load_pos, prenorm_stats, prenorm_T, ensure_loaded, piece_used, load_wd, ffn_gate_up,
     ffn_down, postnorm, rope_tables, mixer_proj, attention, take_piece, wout_tb, final_norm_store, v_take, v_proj, after_tile0_jit, reset_mm8, gu_first_partial, gu_first_rest) = [L[n] for n in globals_needed]
    for t in range(n_tiles):
        hb = t % 2
        cur, prev = t % 2, (t - 1) % 2
        seq, s0 = t // TILES_PER_SEQ, (t % TILES_PER_SEQ) * T
        first = (t % TILES_PER_SEQ == 0)
        has_next = t + 1 < n_tiles
        if t == 0:
            for tb in range(4):
                i = prenorm_stats(hb, tb)
                prenorm_T(i, tb, 0)
        checkpoint("pre1")
        ensure_loaded(piece_used[0] + 1)

        rgen = rope_tables()

        def hook1(p):
            if p >= 2:
                try:
                    next(rgen)
                except StopIteration:
                    pass
            if p == 10:
                for _ in rgen:
                    pass
                if has_next:
                    load_pos(t + 1)
        ffn_gate_up(0, hook1, jit=(t == 0), hook_every=True)
        checkpoint("gu1")
        if has_next and t >= 1:
            load_x(t + 1)
        sl = {}
        for tb in range(4):
            banks = ffn_down(tb)
            postnorm(hb, tb, banks, 0)
            if tb < 2:
                sl[tb] = prenorm_stats(hb, tb)
            if tb == 2:
                prenorm_T(sl[0], 0, 1)
                sl[2] = prenorm_stats(hb, 2)
        prenorm_T(sl[1], 1, 1)
        sl[3] = prenorm_stats(hb, 3)
        prenorm_T(sl[2], 2, 1)
        v_take()
        for tb in range(3):
            v_proj(cur, tb)
        prenorm_T(sl[3], 3, 1)
        v_proj(cur, 3)
        checkpoint("ffn1")
        mixer_proj(cur)
        if t == 0 and has_next:
            after_tile0_jit()
            load_x(1)
        checkpoint("proj")
        attention(cur, prev, first)
        checkpoint("attn")
        s0w = take_piece()
        s1w = take_piece(1)
        reset_mm8()
        sl = {}
        PO = None
        PD = L["DVE"]
        for tb in range(4):
            banks = wout_tb(tb, s0w, s1w)
            postnorm(hb, tb, banks, 1)
            if tb < 2:
                sl[tb] = prenorm_stats(hb, tb, PD if tb == 1 else PO)
        prenorm_T(sl[0], 0, 2)
        sl[2] = prenorm_stats(hb, 2, PO)
        prenorm_T(sl[1], 1, 2)
        sl[3] = prenorm_stats(hb, 3, PD)
        prenorm_T(sl[2], 2, 2)
        if OPT_SPLIT_GU:
            gu_first_partial(1, t == 0)
        prenorm_T(sl[3], 3, 2)
        if OPT_SPLIT_GU:
            gu_first_rest()
        checkpoint("wout")
        nxt = []

        hoist = has_next

        def hook():
            if hoist:
                nxt.append(prenorm_stats((t + 1) % 2, 0))
                nxt.append(prenorm_stats((t + 1) % 2, 1))
        ffn_gate_up(1, hook, jit=(t == 0), skip_first=OPT_SPLIT_GU)
        if hoist:
            prenorm_T(nxt[0], 0, 0)
            prenorm_T(nxt[1], 1, 0)
            nxt.append(prenorm_stats((t + 1) % 2, 2))
            nxt.append(prenorm_stats((t + 1) % 2, 3))
        for tb in range(4):
            banks = ffn_down(tb)
            if hoist and tb < 2:
                prenorm_T(nxt[2 + tb], 2 + tb, 0)
            postnorm(hb, tb, banks, 2)
            final_norm_store(hb, tb, seq, s0)


def make_consts():
    c = np.zeros((128, 264), np.float32)
    c[:, 0:128] = np.eye(128, dtype=np.float32)
    c[:, 128:256] = np.eye(128, dtype=np.float32)[::-1]
    inv_freq = np.power(np.float32(500000.0), -np.arange(8, dtype=np.float32) * np.float32(2.0 / 16.0)).astype(np.float32)
    for p in range(128):
        e = p % 64
        if e < 16:
            c[p, 256] = inv_freq[e % 8]
        c[p, 257] = -1.0 if e < 8 else 1.0
    return c


def kernel(x, positions, ffn1_pre_g, ffn1_w_gate, ffn1_w_up, ffn1_w_down, ffn1_post_g,
           mix_pre_g, w_in, rel_bias_a, sinks_b, w_out, mix_post_g,
           ffn2_pre_g, ffn2_w_gate, ffn2_w_up, ffn2_w_down, ffn2_post_g, final_g):
    f32 = np.float32
    x = np.ascontiguousarray(np.asarray(x, f32))
    positions = np.ascontiguousarray(np.asarray(positions, np.int32))
    gains = np.ascontiguousarray(np.stack([np.asarray(g, f32)[0] for g in (
        ffn1_pre_g, ffn1_post_g, mix_pre_g, mix_post_g, ffn2_pre_g, ffn2_post_g, final_g)], 0))
    shared = {
        "wg0": np.ascontiguousarray(np.asarray(ffn1_w_gate, f32)[0]), "wu0": np.ascontiguousarray(np.asarray(ffn1_w_up, f32)[0]),
        "wd0": np.ascontiguousarray(np.asarray(ffn1_w_down, f32)[0]),
        "wg1": np.ascontiguousarray(np.asarray(ffn2_w_gate, f32)[0]), "wu1": np.ascontiguousarray(np.asarray(ffn2_w_up, f32)[0]),
        "wd1": np.ascontiguousarray(np.asarray(ffn2_w_down, f32)[0]),
        "win": np.ascontiguousarray(np.asarray(w_in, f32)[0]), "wout": np.ascontiguousarray(np.asarray(w_out, f32)[0]),
        "gains": gains, "relb": np.ascontiguousarray(np.asarray(rel_bias_a, f32)[0]),
        "sinks": np.ascontiguousarray(np.asarray(sinks_b, f32)), "consts": make_consts(),
    }
    nc = build()
    in_maps = []
    for c in range(NCORES):
        m = dict(shared)
        m["x"] = x[c * SEQ_PER_CORE:(c + 1) * SEQ_PER_CORE]
        m["pos"] = positions[c * SEQ_PER_CORE:(c + 1) * SEQ_PER_CORE]
        in_maps.append(m)
    res = run_bass_kernel_spmd(nc, in_maps, core_ids=list(range(NCORES)))
    return np.concatenate([r["out"] for r in res.results], axis=0).astype(np.float32)
```

```python
import numpy as np
from contextlib import ExitStack
import concourse.bass as bass
import concourse.mybir as mybir
from concourse.bass_utils import run_bass_kernel_spmd

F32 = mybir.dt.float32
BF16 = mybir.dt.bfloat16
I32 = mybir.dt.int32
AF = mybir.ActivationFunctionType
ALU = mybir.AluOpType

NCORES = 8
SEQ = 2048
D = 1024
DFF = 2816
T = 512
NFC = 22
NDC = 8
SEQ_PER_CORE = 4
TILES_PER_SEQ = SEQ // T
EPS = 1e-6
NEG = -30000.0
WINB_COLS = 3200
C_QA, C_KA, C_QB, C_QBS, C_KB4, C_VA, C_VB = 0, 512, 1024, 1536, 2048, 2560, 3072
TWO_PI = 2.0 * np.pi
CW1 = float(np.float32(6.28125))
CW2 = float(np.float32(TWO_PI - 6.28125))
OPT_FINE_WD = False
OPT_BAL_RECIP = True
OPT_SPLIT_GU = True
PI_LO = 3.1415925


class Ev:
    __slots__ = ("src", "c")

    def __init__(self, src, c):
        self.src = src
        self.c = c


class Res:
    __slots__ = ("name", "w", "r", "excl")

    def __init__(self, name="", excl=False):
        self.name = name
        self.w = None
        self.r = {}
        self.excl = excl


class Eng:
    def __init__(self, name, handle, sem):
        self.name = name
        self.h = handle
        self.sem = sem
        self.cnt = 0
        self.seen = {}
        self.thunks = []


class DSem:
    def __init__(self, name, sem):
        self.name = name
        self.sem = sem
        self.cnt = 0


class K:
    def __init__(self):
        self.engs = []

    def _deps(self, eng, reads, writes):
        need = {}

        def add(ev):
            if ev is None:
                return
            if need.get(ev.src, 0) < ev.c:
                need[ev.src] = ev.c

        for r in reads:
            add(r.w)
        for w in writes:
            add(w.w)
            for ev in w.r.values():
                add(ev)
        for src, c in need.items():
            if eng.seen.get(src, 0) < c:
                eng.seen[src] = c
                eng.h.wait_ge(src.sem, c)

    def op(self, eng, fn, reads=(), writes=()):
        ex = [r for r in reads if r.excl]
        if ex:
            writes = list(writes) + ex
        self._deps(eng, reads, writes)
        eng.cnt += 1
        ev = Ev(eng, eng.cnt)
        fn(eng.h).then_inc(eng.sem, 1)
        for r in reads:
            r.r[eng] = ev
        for w in writes:
            w.w = ev
            w.r = {}
        return ev

    def dma(self, eng, dsem, fn, reads=(), writes=()):
        self._deps(eng, reads, writes)
        dsem.cnt += 16
        ev = Ev(dsem, dsem.cnt)
        fn(eng.h).then_inc(dsem.sem, 16)
        for r in reads:
            r.r[dsem] = ev
        for w in writes:
            w.w = ev
            w.r = {}
        return ev

    def wait_all(self, eng, evs):
        for ev in evs:
            if ev is not None and eng.seen.get(ev.src, 0) < ev.c:
                eng.seen[ev.src] = ev.c
                eng.h.wait_ge(ev.src.sem, ev.c)


class StopBuild(Exception):
    pass


def build(n_tiles=16, stop=None):
    nc = bass.Bass("TRN2", target_bir_lowering=False)
    es = ExitStack()

    def checkpoint(name):
        if stop == name:
            raise StopBuild()
    try:
        _build_body(nc, es, n_tiles, checkpoint)
    except StopBuild:
        pass
    es.close()
    return nc


def _build_body(nc, es, n_tiles, checkpoint):

    def dram(name, shape, dt, kind="Internal"):
        return nc.dram_tensor(name, shape, dt, kind=kind)

    x_d = dram("x", [SEQ_PER_CORE, SEQ, D], F32, "ExternalInput")
    pos_d = dram("pos", [SEQ_PER_CORE, SEQ], I32, "ExternalInput")
    wg_d = [dram(f"wg{i}", [D, DFF], F32, "ExternalInput") for i in range(2)]
    wu_d = [dram(f"wu{i}", [D, DFF], F32, "ExternalInput") for i in range(2)]
    wd_d = [dram(f"wd{i}", [DFF, D], F32, "ExternalInput") for i in range(2)]
    win_d = dram("win", [D, 2304], F32, "ExternalInput")
    wout_d = dram("wout", [D, D], F32, "ExternalInput")
    gains_d = dram("gains", [7, D], F32, "ExternalInput")
    relb_d = dram("relb", [8, 257], F32, "ExternalInput")
    sinks_d = dram("sinks", [1, 8], F32, "ExternalInput")
    consts_d = dram("consts", [128, 264], F32, "ExternalInput")
    out_d = dram("out", [SEQ_PER_CORE, SEQ, D], F32, "ExternalOutput")
    gu_s = [dram(f"gu_s{i}", [11, 128, 4096], BF16) for i in range(2)]
    wd_s = [dram(f"wd_s{i}", [128, NFC * D], BF16) for i in range(2)]
    win_s = dram("win_s", [128, NDC, WINB_COLS], BF16)
    wout_s = dram("wout_s", [128, NDC, D], BF16)
    gt_s = dram("gt_s", [8, 384], F32)

    k = K()
    sem_i = [0]

    def newsem(name):
        sem_i[0] += 1
        return es.enter_context(nc.semaphore(f"{name}_{sem_i[0]}"))

    PE = Eng("pe", nc.tensor, newsem("pe"))
    ACT = Eng("act", nc.scalar, newsem("act"))
    DVE = Eng("dve", nc.vector, newsem("dve"))
    POOL = Eng("pool", nc.gpsimd, newsem("pool"))
    SP = Eng("sp", nc.sync, newsem("sp"))
    engs = [PE, ACT, DVE, POOL, SP]

    def dsem(name):
        return DSem(name, newsem(name))

    def barrier():
        evs = [Ev(e, e.cnt) for e in engs if e.cnt > 0] + [Ev(d, d.cnt) for d in all_dsems if d.cnt > 0]
        for e in engs:
            k.wait_all(e, evs)

    all_dsems = []

    def mk_dsem(name):
        d = dsem(name)
        all_dsems.append(d)
        return d

    ps_all = es.enter_context(nc.psum_tensor("ps_all", [128, 8 * 512], F32))

    class Bank:
        def __init__(self, t, b, w):
            self.t, self.b, self.w = t, b, w

        def __getitem__(self, key):
            p, c = key
            c0 = 0 if c.start is None else c.start
            c1 = self.w if c.stop is None else c.stop
            return self.t[p, self.b * self.w + c0: self.b * self.w + c1]

        def bitcast(self, dt):
            return Bank(self.t.bitcast(dt), self.b, self.w * 2)

    ps = [Bank(ps_all, i, 512) for i in range(8)]
    ps_res = [Res(f"ps{i}", excl=True) for i in range(8)]
    MM_BANKS = [0, 1, 2, 3]
    ST_BANKS = [4, 5, 6, 7]
    TR_BANKS = [4, 5]
    O_BANKS = [0, 1, 2, 3]
    rot = {"mm": 0, "st": 0, "o": 0, "tr": 0, "mm8": 0}

    def pair_ap(b, ncols):
        return ps_all[:, b * 512:(b + 2) * 512].rearrange("p (two n) -> p two n", two=2)[:, :, 0:ncols]

    def next_bank(pool):
        banks = {"mm": MM_BANKS, "st": ST_BANKS, "o": O_BANKS, "tr": TR_BANKS, "mm8": [6, 7, 0, 1, 2, 3]}[pool]
        b = banks[rot[pool] % len(banks)]
        rot[pool] += 1
        return b

    def v3(a, b):
        return lambda tns: tns[:, 0:a * b].rearrange("p (a b) -> p a b", a=a)

    def v3s(a, b, lo, hi):
        return lambda tns: tns[:, 0:a * b].rearrange("p (a b) -> p a b", a=a)[:, :, lo:hi]

    def flat(n):
        return lambda tns: tns[:, 0:n]

    def head_swap_casts(n_heads, src_off, dst_off, W_src, W_dst):
        res = []
        for (d0, s0, n) in ((0, 8, 8), (8, 0, 8), (16, 16, 48)):
            def ov(tns, d0=d0, n=n):
                a = tns[:, 0:8 * W_dst].rearrange("p (a b) -> p a b", a=8)[:, :, dst_off:dst_off + n_heads * 64]
                return a.rearrange("p a (h e) -> p a h e", h=n_heads)[:, :, :, d0:d0 + n]

            def iv(tns, s0=s0, n=n):
                a = tns[:, 0:8 * W_src].rearrange("p (a b) -> p a b", a=8)[:, :, src_off:src_off + n_heads * 64]
                return a.rearrange("p a (h e) -> p a h e", h=n_heads)[:, :, :, s0:s0 + n]
            res.append((ov, iv))
        return res

    wgv = [wg_d[f].rearrange("(dc p) f -> p dc f", p=128) for f in range(2)]
    wuv = [wu_d[f].rearrange("(dc p) f -> p dc f", p=128) for f in range(2)]
    wdv = [wd_d[f].rearrange("(fc p) d -> p fc d", p=128) for f in range(2)]
    winv = win_d.rearrange("(dc p) f -> p dc f", p=128)
    woutv = wout_d.rearrange("(c p) d -> p c d", p=128)
    gt_sem = mk_dsem("gt")
    gt_r = Res()
    with nc.allow_non_contiguous_dma(reason="one-off tiny table build"):
        k.dma(SP, gt_sem, lambda h: h.dma_start(out=gt_s[:, 0:256], in_=relb_d[:, 1:257]), [], [gt_r])
        k.dma(SP, gt_sem, lambda h: h.dma_start(out=bass.AP(gt_s, 256, [[384, 8], [1, 128], [1, 1]]),
                                                in_=bass.AP(relb_d, 256, [[257, 8], [0, 128], [1, 1]])), [], [gt_r])
    barrier()
    checkpoint("prep")

    def sb(name, shape, dt):
        return es.enter_context(nc.sbuf_tensor(name, shape, dt))

    h_t = [sb(f"h{i}", [128, 4, D], F32) for i in range(2)]
    h_r = [[Res(f"h{i}_{tb}") for tb in range(4)] for i in range(2)]
    xn_tok = [sb(f"xntok{i}", [128, D], BF16) for i in range(2)]
    xn_tok_r = [Res() for _ in range(2)]
    xnT = sb("xnT", [128, NDC, T], BF16)
    xnT_r = [Res(f"xnT{tb}") for tb in range(4)]
    U = sb("U", [128, NFC * T], BF16)
    act_v = U[:, :].rearrange("p (a b) -> p a b", a=NFC)
    U_r = [Res(f"U{i}") for i in range(NFC)]
    QaT = U[:, 0:2048].rearrange("p (a b) -> p a b", a=4)
    QbT = U[:, 2048:4096].rearrange("p (a b) -> p a b", a=4)
    OT = U[:, 4096:8192].rearrange("p (a b) -> p a b", a=8)
    PT = [U[:, 8192 + 512 * i: 8192 + 512 * (i + 1)] for i in range(6)]
    QaT_r, QbT_r, OT_r, PT_r = U_r[0:4], U_r[4:8], U_r[8:16], U_r[16:22]
    ring = [sb(f"ring{i}", [128, 4096], BF16) for i in range(3)]
    ring_r = [Res(f"ring{i}") for i in range(3)]
    ring_sem = [mk_dsem(f"ring{i}") for i in range(3)]
    Wd = sb("Wd", [128, NFC * D], BF16)
    Wd_r = Res("Wd")
    wd_sem = mk_dsem("wdl")
    wd_sem_sw = mk_dsem("wdlsw")
    ring_sem_sw = [mk_dsem(f"ringsw{i}") for i in range(3)]
    kv1 = sb("kv1", [128, 4096], F32)
    kv1_r = Res("kv1")
    kv1b = kv1.bitcast(BF16)
    KaT = [sb("KaT0", [128, 4, T], BF16), kv1b[:, 4096:6144].rearrange("p (a b) -> p a b", a=4)]
    KaT_r = [[Res() for _ in range(4)] for _ in range(2)]
    KbT = [sb("KbT0", [128, 2, T], BF16), kv1b[:, 6144:7168].rearrange("p (a b) -> p a b", a=2)]
    KbT_r = [[Res() for _ in range(2)] for _ in range(2)]
    Va = [sb("Va0", [128, 4, 1024], BF16), kv1b[:, 0:4096].rearrange("p (a b) -> p a b", a=4)]
    Va_r = [[Res() for _ in range(4)] for _ in range(2)]
    Vb = [sb(f"Vb{i}", [128, 4, 384], BF16) for i in range(2)]
    Vb_r = [[Res() for _ in range(4)] for _ in range(2)]
    tmpA = [sb(f"tmpA{i}", [128, 1024], F32) for i in range(2)]
    tmpA_r = [Res(f"tmpA{i}") for i in range(2)]
    Ct = sb("Ct", [128, T], F32)
    St = sb("St", [128, T], F32)
    CS_r = Res("CS")
    rsc = sb("rsc", [128, 1024], F32)
    rsc_r = Res("rsc")
    posi = rsc[:, 512:1024].bitcast(I32)
    posi_r = rsc_r
    junk = rsc[:, 0:256].bitcast(BF16)
    junk2 = rsc[:, 0:512].bitcast(BF16)
    junk_r = rsc_r
    pos_sem = mk_dsem("pos")
    gpost = [sb(f"gpost{i}", [128, D], F32) for i in range(4)]
    gpre = sb("gpre", [128, 24], F32)
    BT = sb("BT", [128, 8, 256], BF16)
    ident = sb("ident", [128, 128], BF16)
    antiI = sb("antiI", [128, 128], BF16)
    cst = sb("cst", [128, 8], F32)
    esink = sb("esink", [128, 8], F32)
    epsb = sb("epsb", [128, 1], F32)
    NACC = 0
    NROT = 48
    stats = sb("stats", [128, NACC + NROT], F32)
    rot_r = [Res() for _ in range(NROT)]
    const_r = Res("const")
    x_sem = [mk_dsem(f"xld{i}") for i in range(2)]
    out_sem = [mk_dsem(f"ost{i}") for i in range(2)]
    su_sem = mk_dsem("setup")

    su_res = []

    def setup_dma(out_ap, in_ap, res):
        with nc.allow_non_contiguous_dma(reason="one-off small constant loads"):
            k.dma(SP, su_sem, lambda h: h.dma_start(out=out_ap, in_=in_ap), [], [res])
        su_res.append(res)

    cst_r = Res()
    setup_dma(cst[:, :], consts_d[:, 256:264], cst_r)
    idst_r = h_r[0][0]
    setup_dma(h_t[0][:, 0, 0:256], consts_d[:, 0:256], idst_r)
    gp_r = [Res() for _ in range(4)]
    for i, gi in enumerate((1, 3, 5, 6)):
        setup_dma(gpost[i][:, :], bass.AP(gains_d, gi * D, [[0, 128], [1, D]]), gp_r[i])
    gpre_r = Res()
    with nc.allow_non_contiguous_dma(reason="tiny one-off gain vector transposed load"):
        for i, gi in enumerate((0, 2, 4)):
            k.dma(SP, su_sem, lambda h, i=i, gi=gi: h.dma_start(
                out=bass.AP(gpre, i * 8, [[24, 128], [1, 8], [1, 1]]), in_=bass.AP(gains_d, gi * D, [[1, 128], [128, 8], [1, 1]])), [], [gpre_r])
    su_res.append(gpre_r)
    sk_r = Res()
    setup_dma(esink[:, :], bass.AP(sinks_d, 0, [[0, 128], [1, 8]]), sk_r)
    hank = tmpA
    cc = sb("cc", [128, 8], F32)
    cc_r = Res()
    setup_dma(cc[:, :], bass.AP(relb_d, 256, [[0, 128], [257, 8]]), cc_r)
    hk_r = [Res(), Res()]
    for half in range(2):
        setup_dma(tmpA[half][:, :].rearrange("p (h q) -> p h q", h=4),
                  bass.AP(gt_s, half * 4 * 384, [[1, 128], [384, 4], [1, 256]]), hk_r[half])
    for r_ in su_res:
        r_.w = Ev(su_sem, su_sem.cnt)
    k.op(DVE, lambda h: h.tensor_copy(out=ident[:, :], in_=h_t[0][:, 0, 0:128]), [idst_r], [const_r])
    k.op(DVE, lambda h: h.tensor_copy(out=antiI[:, :], in_=h_t[0][:, 0, 128:256]), [idst_r], [const_r])
    k.op(POOL, lambda h: h.memset(epsb[:, :], EPS), [], [const_r])
    k.op(POOL, lambda h: h.memset(stats[:, :], 0.0), [], [const_r])
    for i in range(2):
        k.op(POOL, lambda h, i=i: h.memset(Va[i][:, :, :], 1.0), [], [r for r in Va_r[i]])
        k.op(POOL, lambda h, i=i: h.memset(Vb[i][:, :, :], 1.0), [], [r for r in Vb_r[i]])
    k.op(ACT, lambda h: h.activation(out=esink[:, :], in_=esink[:, :], func=AF.Exp), [sk_r], [sk_r])
    k.op(DVE, lambda h: h.tensor_scalar(out=gpost[0][:, :], in0=gpost[0][:, :], scalar1=0.5, scalar2=None, op0=ALU.mult), [gp_r[0]], [gp_r[0]])
    k.op(DVE, lambda h: h.tensor_scalar(out=gpost[2][:, :], in0=gpost[2][:, :], scalar1=0.5, scalar2=None, op0=ALU.mult), [gp_r[2]], [gp_r[2]])
    hb16 = PT
    for hd in range(8):
        half, j = hd // 4, hd % 4
        src = tmpA[half][:, j * 256:(j + 1) * 256]
        dst = PT[hd // 2][:, (hd % 2) * 256:(hd % 2 + 1) * 256]
        k.op(DVE, lambda h, src=src, dst=dst, hd=hd: h.tensor_scalar(
            out=dst, in0=src, scalar1=cc[:, hd:hd + 1], scalar2=8.0, op0=ALU.subtract, op1=ALU.mult),
            [hk_r[half], cc_r], [PT_r[hd // 2]])
    for hd in range(8):
        b = next_bank("mm")
        dst = PT[hd // 2][:, (hd % 2) * 256:(hd % 2 + 1) * 256]
        k.op(PE, lambda h, b=b, dst=dst: h.matmul(ps[b][:, 0:256], antiI[:, :], dst, start=True, stop=True),
             [const_r, PT_r[hd // 2]], [ps_res[b]])
        k.op(DVE, lambda h, b=b, hd=hd: h.tensor_copy(out=BT[:, hd, :], in_=ps[b][:, 0:256]), [ps_res[b]], [const_r])
    k.op(POOL, lambda h: h.memset(BT[64:128, :, 0:64], NEG), [const_r], [const_r])
    k.op(ACT, lambda h: h.activation(out=BT[:, :, :], in_=BT[:, :, :], func=AF.Exp, scale=0.125), [const_r], [const_r])
    barrier()
    checkpoint("setup")

    pieces = []
    piece_loaded = [0]
    piece_used = [0]
    scr = {}

    def scr_res(key):
        if key not in scr:
            scr[key] = Res(str(key))
        return scr[key]
    wd_scr_r = [[Res() for _ in range(11)] for _ in range(2)]

    def piece_src(kind, f=None, p=None):
        r_ = scr_res((kind, f, p))
        if kind == "gu":
            return gu_s[f][p], (lambda tns: tns[:, :]), r_
        cols = {"qa": (C_QA, 512), "ka": (C_KA, 512), "qb": (C_QB, 512), "qbs": (C_QBS, 512), "kb4": (C_KB4, 512),
                "va": (C_VA, 512), "vb": (C_VB, 128)}
        if kind in cols:
            c0, w = cols[kind]
            return win_s[:, :, c0:c0 + w], v3(8, w), r_
        if kind == "wo":
            return wout_s[:, :, p * 512:(p + 1) * 512], v3(8, 512), r_
        raise ValueError(kind)

    def tile_piece_list():
        lst = [("gu", 0, p) for p in range(11)]
        lst += [("va", None, None), ("vb", None, None), ("qa", None, None), ("ka", None, None), ("qb", None, None),
                ("qbs", None, None), ("kb4", None, None)]
        lst += [("wo", None, 0), ("wo", None, 1)]
        lst += [("gu", 1, p) for p in range(11)]
        return lst
    NP0 = len(tile_piece_list())

    for t in range(n_tiles):
        for it in tile_piece_list():
            pieces.append(piece_src(*it))

    order0 = [("qbs", None, None), ("kb4", None, None)]
    jit_pos = {it: j for j, it in enumerate(order0)}

    def jit_spec(kind, f, p):
        ident_v = (lambda d: d)
        if kind == "gu":
            lo, hi = p * 256, (p + 1) * 256
            return ([(lambda tns: tns[:, 0:2048].rearrange("p (a b) -> p a b", a=8), wgv[f][:, :, lo:hi]),
                     (lambda tns: tns[:, 2048:4096].rearrange("p (a b) -> p a b", a=8), wuv[f][:, :, lo:hi])],
                    [(flat(4096), flat(4096))], [(gu_s[f][p], flat(4096), scr_res((kind, f, p)))])
        if kind == "wd":
            f0, f1 = 4 * p, min(4 * p + 4, NFC)
            n = (f1 - f0) * D
            return ([(v3(f1 - f0, D), wdv[f][:, f0:f1, :])], [(ident_v, flat(n))],
                    [(wd_s[f][:, f0 * D:f1 * D], ident_v, wd_scr_r[f][p])])
        if kind in ("qa", "ka", "qb", "va"):
            c_src = {"qa": 0, "ka": 512, "qb": 1536, "va": 1024}[kind]
            c_dst = {"qa": C_QA, "ka": C_KA, "qb": C_QB, "va": C_VA}[kind]
            return ([(v3(8, 512), winv[:, :, c_src:c_src + 512])], [(flat(4096), flat(4096))],
                    [(win_s[:, :, c_dst:c_dst + 512], v3(8, 512), scr_res((kind, f, p)))])
        if kind == "qbs":
            return ([(v3(8, 512), winv[:, :, 1536:2048])], head_swap_casts(8, 0, 0, 512, 512),
                    [(win_s[:, :, C_QBS:C_QBS + 512], v3(8, 512), scr_res((kind, f, p)))])
        if kind == "kb4":
            cs_ = [(v3s(8, 512, 0, 128), v3(8, 128))]
            cs_ += head_swap_casts(2, 0, 128, 128, 512)
            cs_ += [(v3s(8, 512, 256, 320), v3s(8, 128, 64, 128)), (v3s(8, 512, 320, 384), v3s(8, 128, 0, 64))]
            cs_ += head_swap_casts(1, 64, 384, 128, 512) + head_swap_casts(1, 0, 448, 128, 512)
            return ([(v3(8, 128), winv[:, :, 2048:2176])], cs_,
                    [(win_s[:, :, C_KB4:C_KB4 + 512], v3(8, 512), scr_res((kind, f, p)))])
        if kind == "vb":
            return ([(v3(8, 128), winv[:, :, 2176:2304])], [(flat(1024), flat(1024))],
                    [(win_s[:, :, C_VB:C_VB + 128], v3(8, 128), scr_res((kind, f, p)))])
        if kind == "wo":
            return ([(v3(8, 512), woutv[:, :, p * 512:(p + 1) * 512])], [(flat(4096), flat(4096))],
                    [(wout_s[:, :, p * 512:(p + 1) * 512], v3(8, 512), scr_res((kind, f, p)))])
        raise ValueError(kind)

    stage_ap = [h_t[1][:, :, :].rearrange("p a b -> p (a b)"), kv1[:, :]]
    stage_res = [h_r[1], [kv1_r]]
    jit_sem = [mk_dsem("jl0"), mk_dsem("jl1")]
    jit_issued = [0]
    slot_store_sem = [mk_dsem(f"sst{i}") for i in range(3)]
    wd_store_sem = [[mk_dsem(f"wst{f}_{c}") for c in range(11)] for f in range(2)]

    def jit_issue_load(j):
        if j >= len(order0) or j < jit_issued[0]:
            return
        assert j == jit_issued[0]
        s_ = j % 2
        loads, _, _ = jit_spec(*order0[j])
        for (sv, src) in loads:
            k.dma(SP, jit_sem[s_], lambda h, sv=sv, src=src: h.dma_start(out=sv(stage_ap[s_]), in_=src), [], stage_res[s_])
        jit_issued[0] += 1

    def jit_materialize(item, dest, dest_res, store_sem):
        j = jit_pos[item]
        jit_issue_load(j)
        jit_issue_load(j + 1)
        s_ = j % 2
        _, casts, stores = jit_spec(*item)
        for (ov, iv) in casts:
            k.op(DVE, lambda h, ov=ov, iv=iv: h.tensor_copy(out=ov(dest), in_=iv(stage_ap[s_])), stage_res[s_], [dest_res])
        jit_issue_load(j + 2)
        for (dst, ov, r_) in stores:
            k.dma(SP, store_sem, lambda h, dst=dst, ov=ov: h.dma_start(out=dst, in_=ov(dest)), [dest_res], [r_])

    def ensure_loaded(upto):
        while piece_loaded[0] <= min(upto, len(pieces) - 1):
            i = piece_loaded[0]
            piece_loaded[0] += 1
            if i < NP0:
                kind, f_, p_ = tile0_items[i]
                if kind in ("qbs", "kb4"):
                    continue
                s = i % 3
                for (view, src32) in cast_srcs(kind, f_, p_):
                    k.dma(POOL, ring_sem_sw[s], lambda h, s=s, view=view, src32=src32: h.dma_start(out=view(ring[s]), in_=src32),
                          [], [ring_r[s]])
                src, view, r_ = pieces[i]
                k.dma(SP, slot_store_sem[s], lambda h, s=s, src=src, view=view: h.dma_start(out=src, in_=view(ring[s])),
                      [ring_r[s]], [r_])
                continue
            s = i % 3
            src, view, r_ = pieces[i]
            k.dma(SP, ring_sem[s], lambda h, s=s, src=src, view=view: h.dma_start(out=view(ring[s]), in_=src), [r_], [ring_r[s]])

    tile0_items = tile_piece_list()

    def cast_srcs(kind, f, p):
        if kind == "gu":
            lo, hi = p * 256, (p + 1) * 256
            return [(lambda tns: tns[:, 0:2048].rearrange("p (a b) -> p a b", a=8), wgv[f][:, :, lo:hi]),
                    (lambda tns: tns[:, 2048:4096].rearrange("p (a b) -> p a b", a=8), wuv[f][:, :, lo:hi])]
        if kind in ("qa", "ka", "qb", "va"):
            c_src = {"qa": 0, "ka": 512, "qb": 1536, "va": 1024}[kind]
            return [(v3(8, 512), winv[:, :, c_src:c_src + 512])]
        if kind == "vb":
            return [(v3(8, 128), winv[:, :, 2176:2304])]
        if kind == "wo":
            return [(v3(8, 512), woutv[:, :, p * 512:(p + 1) * 512])]
        raise ValueError(kind)

    def take_piece(la=2):
        i = piece_used[0]
        piece_used[0] += 1
        if i < NP0 and tile0_items[i][0] in ("qbs", "kb4"):
            jit_materialize(tile0_items[i], ring[i % 3], ring_r[i % 3], slot_store_sem[i % 3])
        ensure_loaded(i + la)
        return i % 3

    stat_i = [0]
    rot_i = [0]

    def acc_col():
        return stat_col()

    def stat_col():
        j = rot_i[0] % NROT
        rot_i[0] += 1
        return stats[:, NACC + j:NACC + j + 1], rot_r[j]

    def rstd_from(ss_ap, ss_r):
        rs_ap, rs_r = stat_col()
        k.op(ACT, lambda h: h.activation(out=rs_ap, in_=ss_ap, func=AF.Sqrt, bias=epsb[:, :], scale=1.0), [ss_r, const_r], [rs_r])
        rd_ap, rd_r = stat_col()
        k.op(DVE, lambda h: h.reciprocal(out=rd_ap, in_=rs_ap), [rs_r], [rd_r])
        return rd_ap, rd_r

    xn_rot = [0]

    def prenorm_stats(hb, tb, scale_eng=None):
        i = xn_rot[0] % 2
        xn_rot[0] += 1
        ss_ap, ss_r = acc_col()
        src = h_t[hb][:, tb, :]
        k.op(ACT, lambda h: h.activation(out=xn_tok[i][:, :], in_=src, func=AF.Square, scale=1.0 / 32.0, accum_out=ss_ap),
             [h_r[hb][tb], const_r], [xn_tok_r[i], ss_r])
        rd_ap, rd_r = rstd_from(ss_ap, ss_r)
        if scale_eng is POOL or scale_eng is DVE:
            k.op(scale_eng, lambda h: h.tensor_scalar(out=xn_tok[i][:, :], in0=src, scalar1=rd_ap, scalar2=None, op0=ALU.mult),
                 [h_r[hb][tb], rd_r], [xn_tok_r[i]])
        else:
            k.op(ACT, lambda h: h.activation(out=xn_tok[i][:, :], in_=src, func=AF.Copy, scale=rd_ap),
                 [h_r[hb][tb], rd_r], [xn_tok_r[i]])
        return i

    def prenorm_T(i, tb, gi):
        b = next_bank("tr")
        pst = ps[b].bitcast(BF16)

        def f(h):
            ins = None
            for dc in range(NDC):
                ins = h.transpose(pst[:, dc * 128:(dc + 1) * 128], xn_tok[i][:, dc * 128:(dc + 1) * 128], ident[:, :])
            return ins
        k.op(PE, f, [xn_tok_r[i], const_r], [ps_res[b]])
        in0 = pst[:, :].rearrange("p (a b) -> p a b", a=NDC)
        in1 = bass.AP(gpre, gi * 8, [[24, 128], [1, 8], [0, 128]])
        k.op(DVE, lambda h: h.tensor_tensor(out=xnT[:, :, tb * 128:(tb + 1) * 128], in0=in0, in1=in1, op=ALU.mult),
             [ps_res[b], gpre_r], [xnT_r[tb]])

    def mm_group(b, out_ap, pairs, reads, start=True):
        def f(h):
            ins = None
            n = len(pairs)
            for j, (l, r) in enumerate(pairs):
                ins = h.matmul(out_ap, l, r, start=(start and j == 0), stop=(j == n - 1))
            return ins
        return k.op(PE, f, reads, [ps_res[b]])

    def postnorm(hb, tb, banks, gp):
        i = tb % 2
        b0, b1 = banks
        assert b1 == b0 + 1
        st_ap, st_r = acc_col()
        fv = ps_all[:, b0 * 512:(b0 + 2) * 512]
        k.op(ACT, lambda h: h.activation(out=junk2, in_=fv, func=AF.Square, scale=1.0 / 32.0, accum_out=st_ap),
             [ps_res[b0], ps_res[b1], const_r], [junk_r, st_r])
        k.op(DVE, lambda h: h.tensor_tensor(out=tmpA[i][:, :], in0=fv, in1=gpost[gp][:, :], op=ALU.mult),
             [ps_res[b0], ps_res[b1], gp_r[gp]], [tmpA_r[i]])
        rd_ap, rd_r = rstd_from(st_ap, st_r)
        dst = h_t[hb][:, tb, :]
        k.op(DVE, lambda h: h.scalar_tensor_tensor(out=dst, in0=tmpA[i][:, :], scalar=rd_ap, in1=dst, op0=ALU.mult, op1=ALU.add),
             [tmpA_r[i], rd_r, h_r[hb][tb]], [h_r[hb][tb]])

    gu_state = {}

    def gu_first_partial(wd_f, jit):
        s = take_piece()
        load_wd(wd_f, 0, first_tile=jit)
        gv = ring[s][:, :].rearrange("p (g a b) -> p g a b", g=2, a=8)
        if rot["mm"] % 2:
            rot["mm"] += 1
        banks = [next_bank("mm") for _ in range(4)]
        n = 0
        for j in range(2):
            for g_ in range(2):
                b = banks[n]
                n += 1
                mm_group(b, ps[b][:, 0:384], [(gv[:, g_, dc, j * 128:(j + 1) * 128], xnT[:, dc, 0:384]) for dc in range(NDC)],
                         [ring_r[s]] + xnT_r[0:3])
        gu_state["s"], gu_state["banks"] = s, banks

    def gu_first_rest():
        s, banks = gu_state["s"], gu_state["banks"]
        gv = ring[s][:, :].rearrange("p (g a b) -> p g a b", g=2, a=8)
        n = 0
        for j in range(2):
            bg, bu = banks[n], banks[n + 1]
            for g_, b in ((0, bg), (1, bu)):
                mm_group(b, ps[b][:, 384:512], [(gv[:, g_, dc, j * 128:(j + 1) * 128], xnT[:, dc, 384:512]) for dc in range(NDC)],
                         [ring_r[s], xnT_r[3]])
            n += 2
            fc = j
            i = fc % 2
            k.op(ACT, lambda h, bg=bg, i=i: h.activation(out=tmpA[i][:, 0:512], in_=ps[bg][:, :], func=AF.Silu),
                 [ps_res[bg]], [tmpA_r[i]])
            k.op(DVE, lambda h, bu=bu, i=i, fc=fc: h.tensor_tensor(out=act_v[:, fc, :], in0=ps[bu][:, :], in1=tmpA[i][:, 0:512],
                                                                   op=ALU.mult), [ps_res[bu], tmpA_r[i]], [U_r[fc]])

    def ffn_gate_up(wd_f, mid_hook=None, jit=False, hook_every=False, skip_first=False):
        for p in range(11):
            if skip_first and p == 0:
                continue
            s = take_piece()
            if jit and OPT_FINE_WD:
                load_wd(wd_f, p, first_tile=True)
            elif p in (0, 2, 4, 6):
                load_wd(wd_f, p // 2, first_tile=jit)
            if mid_hook is not None and hook_every:
                mid_hook(p)
            elif mid_hook is not None and p == 6:
                mid_hook()
            gv = ring[s][:, :].rearrange("p (g a b) -> p g a b", g=2, a=8)
            for j in range(2):
                fc = 2 * p + j
                bg, bu = next_bank("mm"), next_bank("mm")
                mm_group(bg, ps[bg][:, :], [(gv[:, 0, dc, j * 128:(j + 1) * 128], xnT[:, dc, :]) for dc in range(NDC)],
                         [ring_r[s]] + xnT_r)
                mm_group(bu, ps[bu][:, :], [(gv[:, 1, dc, j * 128:(j + 1) * 128], xnT[:, dc, :]) for dc in range(NDC)],
                         [ring_r[s]] + xnT_r)
                i = fc % 2
                k.op(ACT, lambda h, bg=bg, i=i: h.activation(out=tmpA[i][:, 0:512], in_=ps[bg][:, :], func=AF.Silu),
                     [ps_res[bg]], [tmpA_r[i]])
                k.op(DVE, lambda h, bu=bu, i=i, fc=fc: h.tensor_tensor(out=act_v[:, fc, :], in0=ps[bu][:, :], in1=tmpA[i][:, 0:512],
                                                                       op=ALU.mult), [ps_res[bu], tmpA_r[i]], [U_r[fc]])

    def load_wd(f, q, first_tile=False):
        if first_tile and OPT_FINE_WD:
            f0, f1 = 2 * q, 2 * q + 2
        else:
            f0, f1 = q * 6, min((q + 1) * 6, NFC)
        c0, c1 = f0 * D, f1 * D
        if first_tile:
            dstv = Wd[:, c0:c1].rearrange("p (a b) -> p a b", a=f1 - f0)
            k.dma(POOL, wd_sem_sw, lambda h: h.dma_start(out=dstv, in_=wdv[f][:, f0:f1, :]), [], [Wd_r])
            k.dma(SP, wd_store_sem[f][q], lambda h: h.dma_start(out=wd_s[f][:, c0:c1], in_=Wd[:, c0:c1]), [Wd_r], [wd_scr_r[f][q]])
        else:
            k.dma(SP, wd_sem, lambda h: h.dma_start(out=Wd[:, c0:c1], in_=wd_s[f][:, c0:c1]), wd_scr_r[f], [Wd_r])

    def ffn_down(tb):
        banks = []
        if rot["mm"] % 2:
            rot["mm"] += 1
        for hf in range(2):
            b = next_bank("mm")
            mm_group(b, ps[b][:, :], [(act_v[:, fc, tb * 128:(tb + 1) * 128], Wd[:, fc * D + hf * 512: fc * D + (hf + 1) * 512])
                                      for fc in range(NFC)], [Wd_r] + U_r)
            banks.append(b)
        return banks

    def rope_tables():
        ang, kq = rsc[:, 0:512], rsc[:, 512:1024]
        invf, sgn = cst[:, 0:1], cst[:, 1:2]
        rr, cs = [rsc_r], [CS_r]
        k.op(DVE, lambda h: h.tensor_copy(out=kq, in_=posi), [posi_r], rr)
        k.op(DVE, lambda h: h.tensor_scalar(out=ang, in0=kq, scalar1=invf, scalar2=None, op0=ALU.mult), rr + [cst_r], rr)
        yield
        k.op(DVE, lambda h: h.tensor_scalar(out=St[:, :], in0=ang, scalar1=float(1.0 / TWO_PI), scalar2=None, op0=ALU.mult), rr + cs, cs)
        k.op(DVE, lambda h: h.tensor_copy(out=kq.bitcast(I32), in_=St[:, :]), rr + cs, rr)
        k.op(DVE, lambda h: h.tensor_copy(out=St[:, :], in_=kq.bitcast(I32)), rr + cs, cs)
        yield
        k.op(DVE, lambda h: h.scalar_tensor_tensor(out=Ct[:, :], in0=St[:, :], scalar=-CW1, in1=ang, op0=ALU.mult, op1=ALU.add), rr + cs, cs)
        k.op(DVE, lambda h: h.scalar_tensor_tensor(out=Ct[:, :], in0=St[:, :], scalar=-CW2, in1=Ct[:, :], op0=ALU.mult, op1=ALU.add), cs, cs)
        yield
        k.op(DVE, lambda h: h.tensor_scalar(out=Ct[:, :], in0=Ct[:, :], scalar1=-PI_LO, scalar2=PI_LO, op0=ALU.max, op1=ALU.min), cs, cs)
        k.op(ACT, lambda h: h.activation(out=St[:, :], in_=Ct[:, :], func=AF.Sin, scale=sgn), cs + [cst_r], cs)
        yield
        k.op(DVE, lambda h: h.tensor_scalar(out=ang, in0=ang, scalar1=float(0.5 * np.pi), scalar2=None, op0=ALU.add), rr, rr)
        k.op(DVE, lambda h: h.tensor_scalar(out=Ct[:, :], in0=ang, scalar1=float(1.0 / TWO_PI), scalar2=None, op0=ALU.mult), rr + cs, cs)
        yield
        k.op(DVE, lambda h: h.tensor_copy(out=kq.bitcast(I32), in_=Ct[:, :]), rr + cs, rr)
        k.op(DVE, lambda h: h.tensor_copy(out=kq, in_=kq.bitcast(I32)), rr, rr)
        yield
        k.op(DVE, lambda h: h.scalar_tensor_tensor(out=Ct[:, :], in0=kq, scalar=-CW1, in1=ang, op0=ALU.mult, op1=ALU.add), rr + cs, cs)
        k.op(DVE, lambda h: h.scalar_tensor_tensor(out=Ct[:, :], in0=kq, scalar=-CW2, in1=Ct[:, :], op0=ALU.mult, op1=ALU.add), rr + cs, cs)
        yield
        k.op(DVE, lambda h: h.tensor_scalar(out=Ct[:, :], in0=Ct[:, :], scalar1=-PI_LO, scalar2=PI_LO, op0=ALU.max, op1=ALU.min), cs, cs)
        k.op(ACT, lambda h: h.activation(out=Ct[:, :], in_=Ct[:, :], func=AF.Sin), cs, cs)
        yield

    def proj_block(s, width, blk):
        v = ring[s][:, 0:8 * width].rearrange("p (a b) -> p a b", a=8)
        b = next_bank("mm")
        mm_group(b, ps[b][:, :], [(v[:, dc, blk * 128:(blk + 1) * 128], xnT[:, dc, :]) for dc in range(NDC)], [ring_r[s]] + xnT_r)
        return b

    def rope_apply(bx, by, dst_ap, dst_r, i):
        k.op(DVE, lambda h: h.tensor_tensor(out=tmpA[i][:, 0:512], in0=ps[bx][:, :], in1=Ct[:, :], op=ALU.mult),
             [ps_res[bx], CS_r], [tmpA_r[i]])
        k.op(DVE, lambda h: h.tensor_tensor(out=tmpA[i][:, 512:1024], in0=ps[by][:, :], in1=St[:, :], op=ALU.mult),
             [ps_res[by], CS_r, tmpA_r[i]], [tmpA_r[i]])
        k.op(POOL, lambda h: h.tensor_tensor(out=dst_ap, in0=tmpA[i][:, 0:512], in1=tmpA[i][:, 512:1024], op=ALU.add),
             [tmpA_r[i]], [dst_r])

    def mixer_proj(cur):
        s = take_piece()
        for blk in range(4):
            b = proj_block(s, 512, blk)
            k.op(ACT, lambda h, b=b, blk=blk: h.copy(out=QaT[:, blk, :], in_=ps[b][:, :]), [ps_res[b]], [QaT_r[blk]])
        s = take_piece()
        for blk in range(4):
            b = proj_block(s, 512, blk)
            k.op(DVE, lambda h, b=b, blk=blk: h.tensor_copy(out=KaT[cur][:, blk, :], in_=ps[b][:, :]), [ps_res[b]], [KaT_r[cur][blk]])
        s1 = take_piece()
        s2 = take_piece(1)
        for blk in range(4):
            bx = proj_block(s1, 512, blk)
            by = proj_block(s2, 512, blk)
            rope_apply(bx, by, QbT[:, blk, :], QbT_r[blk], blk % 2)
        s = take_piece()
        for v in range(2):
            bx = proj_block(s, 512, 2 * v)
            by = proj_block(s, 512, 2 * v + 1)
            rope_apply(bx, by, KbT[cur][:, v, :], KbT_r[cur][v], v)

    vslots = {}

    def v_take():
        vslots["a"] = take_piece()
        vslots["b"] = take_piece(1)

    def v_proj(cur, tb):
        sva, svb = vslots["a"], vslots["b"]
        vva = ring[sva][:, 0:4096].rearrange("p (a b) -> p a b", a=8)
        vvb = ring[svb][:, 0:1024].rearrange("p (a b) -> p a b", a=8)
        ba = next_bank("mm")
        mm_group(ba, ps[ba][:, :], [(xnT[:, dc, tb * 128:(tb + 1) * 128], vva[:, dc, :]) for dc in range(NDC)],
                 [ring_r[sva], xnT_r[tb]])
        bb = next_bank("mm")
        mm_group(bb, ps[bb][:, 0:128], [(xnT[:, dc, tb * 128:(tb + 1) * 128], vvb[:, dc, :]) for dc in range(NDC)],
                 [ring_r[svb], xnT_r[tb]])
        src = ps[ba][:, :].rearrange("p (h two e) -> p h two e", h=4, two=2)
        dstv = Va[cur][:, tb, :].rearrange("p (h four e) -> p h four e", h=4, four=4)
        k.op(ACT, lambda h: h.copy(out=dstv[:, :, 0, :], in_=src[:, :, 0, :]), [ps_res[ba]], [Va_r[cur][tb]])
        k.op(DVE, lambda h: h.tensor_copy(out=dstv[:, :, 3, :], in_=src[:, :, 1, :]), [ps_res[ba]], [Va_r[cur][tb]])
        srcb = ps[bb][:, 0:128].rearrange("p (g e) -> p g e", g=2)
        dstb = Vb[cur][:, tb, :].rearrange("p (g three e) -> p g three e", g=2, three=3)[:, :, 1, :]
        k.op(DVE, lambda h: h.tensor_copy(out=dstb, in_=srcb), [ps_res[bb]], [Vb_r[cur][tb]])

    pt_rot = [0]

    def attention(cur, prev, first):
        steps = []
        for hd in range(8):
            hbk, pb = hd // 2, (hd % 2) * 64
            kts = [4, 5, 6, 7] if first else [3, 0, 1, 2, 4, 5, 6, 7]
            for n, kt in enumerate(kts):
                buf = cur if kt >= 4 else prev
                tbk = kt % 4
                qlo, qhi = max(0, 2 * kt - 8), min(7, 2 * kt + 1)
                extra = []
                if kt >= 3:
                    ilo, ihi = max(0, 2 * kt - 8), min(7, 2 * kt - 5)
                    extra.append((BT[:, hd, (ilo - (2 * kt - 8)) * 64:(ihi - (2 * kt - 8) + 1) * 64], (ilo - qlo) * 64, (ihi - qlo + 1) * 64))
                zero = []
                if 2 * kt + 1 <= 7:
                    i_ = 2 * kt + 1
                    zero.append((slice(0, 64), (i_ - qlo) * 64))
                steps.append(dict(zero=zero,
                    K=KaT[buf][pb:pb + 64, hbk, tbk * 128:(tbk + 1) * 128], K_r=KaT_r[buf][hbk],
                    Q=QaT[pb:pb + 64, hbk, qlo * 64:(qhi + 1) * 64], Q_r=QaT_r[hbk],
                    V=Va[buf][:, tbk, hd * 128:(hd + 1) * 128], V_r=Va_r[buf][tbk],
                    q0=qlo * 64, ncols=(qhi + 1 - qlo) * 64, extra=extra, head=("a", hd), first=(n == 0), last=(n == len(kts) - 1),
                    odd=hd % 2, ot=OT[pb:pb + 64, hbk, :], ot_r=OT_r[hbk], sink=None))
        for hd in range(8):
            hbk, pb, g = hd // 2, (hd % 2) * 64, hd // 4
            var = 0 if pb == g * 64 else 1
            kts = [4, 5, 6, 7] if first else [3, 4, 5, 6, 7]
            for n, kt in enumerate(kts):
                buf = cur if kt >= 4 else prev
                tbk = kt % 4
                ilo, ihi = max(0, 2 * kt - 8), min(7, 2 * kt - 5)
                extra = []
                zero = []
                if 2 * kt - 8 >= 0:
                    zero.append((slice(64, 128), 0))
                if 2 * kt - 5 <= 7:
                    zero.append((slice(0, 64), (2 * kt - 5 - ilo) * 64))
                voff = g * 192 + (64 if hd % 2 == 0 else 0)
                steps.append(dict(zero=zero,
                    K=KbT[buf][pb:pb + 64, var, tbk * 128:(tbk + 1) * 128], K_r=KbT_r[buf][var],
                    Q=QbT[pb:pb + 64, hbk, ilo * 64:(ihi + 1) * 64], Q_r=QbT_r[hbk],
                    V=Vb[buf][:, tbk, voff:voff + 128], V_r=Vb_r[buf][tbk],
                    q0=ilo * 64, ncols=(ihi + 1 - ilo) * 64, extra=extra, head=("b", hd), first=(n == 0), last=(n == len(kts) - 1),
                    odd=hd % 2, ot=OT[pb:pb + 64, 4 + hbk, :], ot_r=OT_r[4 + hbk], sink=hd))

        def interleave(lst):
            out = []
            by_head = {}
            order = []
            for sp in lst:
                if sp["head"] not in by_head:
                    by_head[sp["head"]] = []
                    order.append(sp["head"])
                by_head[sp["head"]].append(sp)
            for j in range(0, len(order), 2):
                a, b = by_head[order[j]], by_head[order[j + 1]]
                for x, y in zip(a, b):
                    out += [x, y]
            return out
        steps = interleave(steps)
        cur_o = {}

        pair_rot = [0]

        def emit_st_pair(spa, spb):
            bA = ST_BANKS[(pair_rot[0] % 2) * 2]
            piA = (pair_rot[0] % 3) * 2
            pair_rot[0] += 1
            n = spa["ncols"]
            assert spb["ncols"] == n
            for sp, b, pi in ((spa, bA, piA), (spb, bA + 1, piA + 1)):
                sp["stb"], sp["pt"] = b, pi
                k.op(PE, lambda h, sp=sp, b=b: h.matmul(ps[b][:, 0:n], sp["K"], sp["Q"], start=True, stop=True),
                     [sp["K_r"], sp["Q_r"]], [ps_res[b]])
            outv = U[:, 8192 + 512 * piA: 8192 + 512 * (piA + 2)].rearrange("p (two n) -> p two n", two=2)[:, :, 0:n]
            k.op(ACT, lambda h: h.activation(out=outv, in_=pair_ap(bA, n), func=AF.Exp, scale=0.125),
                 [ps_res[bA], ps_res[bA + 1]], [PT_r[piA], PT_r[piA + 1]])
            for sp in (spa, spb):
                pi = sp["pt"]
                for (tab, c0, c1) in sp["extra"]:
                    k.op(POOL, lambda h, tab=tab, c0=c0, c1=c1, pi=pi: h.tensor_tensor(out=PT[pi][:, c0:c1], in0=PT[pi][:, c0:c1], in1=tab, op=ALU.mult),
                         [PT_r[pi], const_r], [PT_r[pi]])
                for (rows, c0) in sp["zero"]:
                    k.op(POOL, lambda h, rows=rows, c0=c0, pi=pi: h.memset(PT[pi][rows, c0:c0 + 64], 0.0), [PT_r[pi]], [PT_r[pi]])

        def emit_pv(sp):
            if sp["first"]:
                cur_o[sp["head"]] = next_bank("o")
            ob = cur_o[sp["head"]]
            pi = sp["pt"]
            k.op(PE, lambda h: h.matmul(ps[ob][:, sp["q0"]:sp["q0"] + sp["ncols"]], sp["V"], PT[pi][:, 0:sp["ncols"]],
                                        start=sp["first"], stop=sp["last"], skip_group_check=True),
                 [sp["V_r"], PT_r[pi]], [ps_res[ob]])
            if sp["last"]:
                den = slice(0, 64) if sp["odd"] else slice(64, 128)
                dat = slice(64, 128) if sp["odd"] else slice(0, 64)
                i = sp["head"][1] % 2
                rec = tmpA[i][den, 0:512]
                use_act = (sp["sink"] is not None) or (OPT_BAL_RECIP and sp["head"][1] % 2 == 1)
                if sp["sink"] is not None:
                    hd = sp["sink"]
                    if use_act:
                        k.op(ACT, lambda h: h.activation(out=rec, in_=ps[ob][den, :], func=AF.Ln, bias=esink[den, hd:hd + 1], scale=1.0),
                             [ps_res[ob], sk_r], [tmpA_r[i]])
                        k.op(ACT, lambda h: h.activation(out=rec, in_=rec, func=AF.Exp, scale=-1.0), [tmpA_r[i]], [tmpA_r[i]])
                    else:
                        k.op(DVE, lambda h: h.tensor_scalar(out=rec, in0=ps[ob][den, :], scalar1=esink[den, hd:hd + 1], scalar2=None,
                                                            op0=ALU.add), [ps_res[ob], sk_r], [tmpA_r[i]])
                        k.op(DVE, lambda h: h.reciprocal(out=rec, in_=rec), [tmpA_r[i]], [tmpA_r[i]])
                elif use_act:
                    k.op(ACT, lambda h: h.activation(out=rec, in_=ps[ob][den, :], func=AF.Ln), [ps_res[ob]], [tmpA_r[i]])
                    k.op(ACT, lambda h: h.activation(out=rec, in_=rec, func=AF.Exp, scale=-1.0), [tmpA_r[i]], [tmpA_r[i]])
                else:
                    k.op(DVE, lambda h: h.reciprocal(out=rec, in_=ps[ob][den, :]), [ps_res[ob]], [tmpA_r[i]])
                k.op(DVE, lambda h: h.tensor_tensor(out=sp["ot"], in0=ps[ob][dat, :], in1=rec, op=ALU.mult),
                     [ps_res[ob], tmpA_r[i]], [sp["ot_r"]])

        LAG = 4
        for n in range(0, len(steps), 2):
            emit_st_pair(steps[n], steps[n + 1])
            if n >= LAG:
                emit_pv(steps[n - LAG])
                emit_pv(steps[n - LAG + 1])
        for sp in steps[len(steps) - LAG:]:
            emit_pv(sp)

    def reset_mm8():
        rot["mm8"] = 0

    def wout_tb(tb, s0, s1):
        banks = []
        for hf, s in enumerate((s0, s1)):
            v = ring[s][:, 0:4096].rearrange("p (a b) -> p a b", a=8)
            b = next_bank("mm8")
            mm_group(b, ps[b][:, :], [(OT[:, c, tb * 128:(tb + 1) * 128], v[:, c, :]) for c in range(8)], [ring_r[s]] + OT_r)
            banks.append(b)
        return banks

    def final_norm_store(hb, tb, seq, s0):
        ss_ap, ss_r = acc_col()
        src = h_t[hb][:, tb, :]
        k.op(ACT, lambda h: h.activation(out=tmpA[tb % 2][:, :], in_=src, func=AF.Square, scale=1.0 / 32.0, accum_out=ss_ap),
             [h_r[hb][tb], const_r], [tmpA_r[tb % 2], ss_r])
        rd_ap, rd_r = rstd_from(ss_ap, ss_r)
        k.op(DVE, lambda h: h.scalar_tensor_tensor(out=src, in0=src, scalar=rd_ap, in1=gpost[3][:, :], op0=ALU.mult, op1=ALU.mult),
             [h_r[hb][tb], rd_r, gp_r[3]], [h_r[hb][tb]])
        dst = out_d[seq, s0 + tb * 128: s0 + (tb + 1) * 128, :]
        k.dma(SP, out_sem[hb], lambda h: h.dma_start(out=dst, in_=src), [h_r[hb][tb]], [])

    def load_x(t):
        hb = t % 2
        seq, s0 = t // TILES_PER_SEQ, (t % TILES_PER_SEQ) * T
        srcv = x_d[seq, s0:s0 + T, :].rearrange("(tb p) d -> p tb d", p=128)
        k.dma(SP, x_sem[hb], lambda h: h.dma_start(out=h_t[hb][:, :, :], in_=srcv), [], h_r[hb])

    def load_pos(t):
        seq, s0 = t // TILES_PER_SEQ, (t % TILES_PER_SEQ) * T
        k.dma(SP, pos_sem, lambda h: h.dma_start(out=posi, in_=bass.AP(pos_d, seq * SEQ + s0, [[0, 128], [1, T]])), [], [posi_r])

    def after_tile0_jit():
        for r_ in Va_r[1] + KaT_r[1] + KbT_r[1]:
            r_.w = kv1_r.w
            r_.r = dict(kv1_r.r)
        k.op(POOL, lambda h: h.memset(Va[1][:, :, :], 1.0), [], Va_r[1])

    load_x(0)
    load_pos(0)
    if n_tiles > 1:
        pass
    try:
        _tile_loop(locals())
    except StopBuild:
        dstv = out_d[0, 0:T, :].rearrange("(tb p) d -> p tb d", p=128)
        k.dma(SP, out_sem[0], lambda h: h.dma_start(out=dstv, in_=h_t[0][:, :, :]), h_r[0], [])
        flat1 = h_t[1][:, :, :].rearrange("p a b -> p (a b)")
        k.op(DVE, lambda h: h.tensor_copy(out=flat1, in_=U[:, 4096:8192]), OT_r, h_r[1])
        k.dma(SP, out_sem[0], lambda h: h.dma_start(out=out_d[1, 0:T, :].rearrange("(p a) d -> p (a d)", a=4), in_=flat1), h_r[1], [])
        k.op(DVE, lambda h: h.tensor_copy(out=flat1, in_=U[:, 0:4096]), QaT_r + QbT_r, h_r[1])
        k.dma(SP, out_sem[0], lambda h: h.dma_start(out=out_d[2, 0:T, :].rearrange("(p a) d -> p (a d)", a=4), in_=flat1), h_r[1], [])
        k.op(DVE, lambda h: h.tensor_copy(out=flat1[:, 0:2048], in_=KaT[0][:, :, :].rearrange("p a b -> p (a b)")), KaT_r[0], h_r[1])
        k.op(DVE, lambda h: h.tensor_copy(out=flat1[:, 2048:3072], in_=KbT[0][:, :, :].rearrange("p a b -> p (a b)")), KbT_r[0], h_r[1])
        k.op(DVE, lambda h: h.tensor_copy(out=flat1[:, 3072:4096], in_=Va[0][:, 0, :]), Va_r[0], h_r[1])
        k.dma(SP, out_sem[0], lambda h: h.dma_start(out=out_d[3, 0:T, :].rearrange("(p a) d -> p (a d)", a=4), in_=flat1), h_r[1], [])
    k.wait_all(SP, [Ev(d, d.cnt) for d in out_sem if d.cnt > 0])


def _tile_loop(L):
    globals_needed = ("n_tiles checkpoint load_x load_pos prenorm_stats prenorm_T ensure_loaded piece_used load_wd ffn_gate_up "
                      "ffn_down postnorm rope_tables mixer_proj attention take_piece wout_tb final_norm_store v_take v_proj after_tile0_jit reset_mm8 gu_first_partial gu_first_rest").split()
    (n_tiles, checkpoint, load_x, load_pos, prenorm_stats, prenorm_T, ensure_loaded, piece_used, load_wd, ffn_gate_up,
     ffn_down, postnorm, rope_tables, mixer_proj, attention, take_piece, wout_tb, final_norm_store, v_take, v_proj, after_tile0_jit, reset_mm8, gu_first_partial, gu_first_rest) = [L[n] for n in globals_needed]
    for t in range(n_tiles):
        hb = t % 2
        cur, prev = t % 2, (t - 1) % 2
        seq, s0 = t // TILES_PER_SEQ, (t % TILES_PER_SEQ) * T
        first = (t % TILES_PER_SEQ == 0)
        has_next = t + 1 < n_tiles
        if t == 0:
            for tb in range(4):
                i = prenorm_stats(hb, tb)
                prenorm_T(i, tb, 0)
        checkpoint("pre1")
        ensure_loaded(piece_used[0] + 1)

        rgen = rope_tables()

        def hook1(p):
            if p >= 2:
                try:
                    next(rgen)
                except StopIteration:
                    pass
            if p == 10:
                for _ in rgen:
                    pass
                if has_next:
                    load_pos(t + 1)
        ffn_gate_up(0, hook1, jit=(t == 0), hook_every=True)
        checkpoint("gu1")
        if has_next and t >= 1:
            load_x(t + 1)
        sl = {}
        for tb in range(4):
            banks = ffn_down(tb)
            postnorm(hb, tb, banks, 0)
            if tb < 2:
                sl[tb] = prenorm_stats(hb, tb)
            if tb == 2:
                prenorm_T(sl[0], 0, 1)
                sl[2] = prenorm_stats(hb, 2)
        prenorm_T(sl[1], 1, 1)
        sl[3] = prenorm_stats(hb, 3)
        prenorm_T(sl[2], 2, 1)
        v_take()
        for tb in range(3):
            v_proj(cur, tb)
        prenorm_T(sl[3], 3, 1)
        v_proj(cur, 3)
        checkpoint("ffn1")
        mixer_proj(cur)
        if t == 0 and has_next:
            after_tile0_jit()
            load_x(1)
        checkpoint("proj")
        attention(cur, prev, first)
        checkpoint("attn")
        s0w = take_piece()
        s1w = take_piece(1)
        reset_mm8()
        sl = {}
        PO = None
        PD = L["DVE"]
        for tb in range(4):
            banks = wout_tb(tb, s0w, s1w)
            postnorm(hb, tb, banks, 1)
            if tb < 2:
                sl[tb] = prenorm_stats(hb, tb, PD if tb == 1 else PO)
        prenorm_T(sl[0], 0, 2)
        sl[2] = prenorm_stats(hb, 2, PO)
        prenorm_T(sl[1], 1, 2)
        sl[3] = prenorm_stats(hb, 3, PD)
        prenorm_T(sl[2], 2, 2)
        if OPT_SPLIT_GU:
            gu_first_partial(1, t == 0)
        prenorm_T(sl[3], 3, 2)
        if OPT_SPLIT_GU:
            gu_first_rest()
        checkpoint("wout")
        nxt = []

        hoist = has_next

        def hook():
            if hoist:
                nxt.append(prenorm_stats((t + 1) % 2, 0, L["DVE"]))
                nxt.append(prenorm_stats((t + 1) % 2, 1, L["DVE"]))
        ffn_gate_up(1, hook, jit=(t == 0), skip_first=OPT_SPLIT_GU)
        if hoist:
            prenorm_T(nxt[0], 0, 0)
            prenorm_T(nxt[1], 1, 0)
            nxt.append(prenorm_stats((t + 1) % 2, 2))
            nxt.append(prenorm_stats((t + 1) % 2, 3))
        for tb in range(4):
            banks = ffn_down(tb)
            if hoist and tb < 2:
                prenorm_T(nxt[2 + tb], 2 + tb, 0)
            postnorm(hb, tb, banks, 2)
            final_norm_store(hb, tb, seq, s0)


def make_consts():
    c = np.zeros((128, 264), np.float32)
    c[:, 0:128] = np.eye(128, dtype=np.float32)
    c[:, 128:256] = np.eye(128, dtype=np.float32)[::-1]
    inv_freq = np.power(np.float32(500000.0), -np.arange(8, dtype=np.float32) * np.float32(2.0 / 16.0)).astype(np.float32)
    for p in range(128):
        e = p % 64
        if e < 16:
            c[p, 256] = inv_freq[e % 8]
        c[p, 257] = -1.0 if e < 8 else 1.0
    return c


def kernel(x, positions, ffn1_pre_g, ffn1_w_gate, ffn1_w_up, ffn1_w_down, ffn1_post_g,
           mix_pre_g, w_in, rel_bias_a, sinks_b, w_out, mix_post_g,
           ffn2_pre_g, ffn2_w_gate, ffn2_w_up, ffn2_w_down, ffn2_post_g, final_g):
    f32 = np.float32
    x = np.ascontiguousarray(np.asarray(x, f32))
    positions = np.ascontiguousarray(np.asarray(positions, np.int32))
    gains = np.ascontiguousarray(np.stack([np.asarray(g, f32)[0] for g in (
        ffn1_pre_g, ffn1_post_g, mix_pre_g, mix_post_g, ffn2_pre_g, ffn2_post_g, final_g)], 0))
    shared = {
        "wg0": np.ascontiguousarray(np.asarray(ffn1_w_gate, f32)[0]), "wu0": np.ascontiguousarray(np.asarray(ffn1_w_up, f32)[0]),
        "wd0": np.ascontiguousarray(np.asarray(ffn1_w_down, f32)[0]),
        "wg1": np.ascontiguousarray(np.asarray(ffn2_w_gate, f32)[0]), "wu1": np.ascontiguousarray(np.asarray(ffn2_w_up, f32)[0]),
        "wd1": np.ascontiguousarray(np.asarray(ffn2_w_down, f32)[0]),
        "win": np.ascontiguousarray(np.asarray(w_in, f32)[0]), "wout": np.ascontiguousarray(np.asarray(w_out, f32)[0]),
        "gains": gains, "relb": np.ascontiguousarray(np.asarray(rel_bias_a, f32)[0]),
        "sinks": np.ascontiguousarray(np.asarray(sinks_b, f32)), "consts": make_consts(),
    }
    nc = build()
    in_maps = []
    for c in range(NCORES):
        m = dict(shared)
        m["x"] = x[c * SEQ_PER_CORE:(c + 1) * SEQ_PER_CORE]
        m["pos"] = positions[c * SEQ_PER_CORE:(c + 1) * SEQ_PER_CORE]
        in_maps.append(m)
    res = run_bass_kernel_spmd(nc, in_maps, core_ids=list(range(NCORES)))
    return np.concatenate([r["out"] for r in res.results], axis=0).astype(np.float32)
```
